# Optimizing a Trainium2 kernel written in Bass

```python
import math
import jax, jax.numpy as jnp
from jax import lax
import numpy as np

D_MODEL = 2048
BATCH = 16
SEQ = 2048
DEPTH = 2

N_MIXERS = 2
POOL_WINDOWS = (2, 4, 8, 16)
N_POOL_GROUPS = 4
POOL_GROUP_DIM = D_MODEL // N_POOL_GROUPS
ATTN_PATTERNS = ((128, 1), (512, 4), (2048, 16))
N_ATTN_GROUPS = 3
HEAD_DIM = 128
N_SLOTS = 8
GROUP_WIDTH = N_SLOTS * HEAD_DIM
QKV_WIDTH = 3 * N_ATTN_GROUPS * GROUP_WIDTH
BLOCK = 128
N_TOTAL_HEADS = N_ATTN_GROUPS * N_SLOTS
D_FF = 4 * D_MODEL
LN_EPS = 1e-5
DEEPNORM_ALPHA = (2 * DEPTH) ** 0.25
DEEPNORM_BETA = (8 * DEPTH) ** -0.25
N_POOL_LAYERS = (DEPTH + 1) // 2
N_ATTN_LAYERS = DEPTH // 2

kernel_name = "hybrid_pool_dilated_attn_deepnorm"


def layer_norm(x, g, b):
    xf = x.astype(jnp.float32)
    mu = jnp.mean(xf, axis=-1, keepdims=True)
    xc = xf - mu
    var = jnp.mean(xc * xc, axis=-1, keepdims=True)
    y = xc * lax.rsqrt(var + LN_EPS) * g.astype(jnp.float32) + b.astype(jnp.float32)
    return y.astype(x.dtype)


def alibi_slopes():
    n = N_TOTAL_HEADS
    return jnp.exp2(-8.0 * jnp.arange(1, n + 1, dtype=jnp.float32) / n).reshape(N_ATTN_GROUPS, N_SLOTS)


def pool_mixer(x, w_in, w_group, scale, w_out):
    B, S, D = x.shape
    u = (x @ w_in).reshape(B, S, N_POOL_GROUPS, POOL_GROUP_DIM)
    uf = u.astype(jnp.float32)
    cs = jnp.cumsum(uf, axis=1)
    t = jnp.arange(S)
    outs = []
    for g, w in enumerate(POOL_WINDOWS):
        c = cs[:, :, g]
        lagged = jnp.pad(c, ((0, 0), (w, 0), (0, 0)))[:, :S]
        cnt = jnp.minimum(t + 1, w).astype(jnp.float32)[:, None]
        outs.append((c - lagged) / cnt - uf[:, :, g])
    p = jnp.stack(outs, axis=2).astype(x.dtype)
    y = jnp.einsum('bsgc,gcd->bsgd', p, w_group).reshape(B, S, D) * scale
    return y @ w_out


def strided_window_attention(q, k, v, dil, n_back, slopes):
    B, S, H, E = q.shape
    L = S // dil
    nb = -(-L // BLOCK)
    Lp = nb * BLOCK
    N = B * dil

    def to_blocks(a):
        a = a.reshape(B, L, dil, H, E).transpose(0, 2, 1, 3, 4).reshape(N, L, H, E)
        a = jnp.pad(a, ((0, 0), (0, Lp - L), (0, 0), (0, 0)))
        return a.reshape(N, nb, BLOCK, H, E)

    def with_prev(a):
        prev = jnp.pad(a, ((0, 0), (1, 0), (0, 0), (0, 0), (0, 0)))[:, :nb]
        return jnp.concatenate([prev, a], axis=2)

    def from_blocks(a):
        rest = a.shape[3:]
        a = a.reshape((B, dil, Lp) + rest)[:, :, :L]
        a = jnp.moveaxis(a, 1, 2)
        return a.reshape((B, S) + rest)

    qb = to_blocks(q)
    kw = with_prev(to_blocks(k))
    vw = with_prev(to_blocks(v))
    s = jnp.einsum('nbqhe,nbkhe->nbhqk', qb, kw, preferred_element_type=jnp.float32)
    s = s * (1.0 / math.sqrt(E))
    qi = jnp.arange(BLOCK)[:, None]
    ki = jnp.arange(2 * BLOCK)[None, :]
    dist = qi + BLOCK - ki
    blk = jnp.arange(nb)[:, None, None]
    valid = ((dist >= 0) & (dist <= n_back))[None] & ((blk > 0) | (ki[None] >= BLOCK))
    bias = -slopes[:, None, None] * (dist * dil).astype(jnp.float32)[None]
    s = s + bias[None, None]
    s = jnp.where(valid[None, :, None], s, -jnp.inf)
    m = jnp.max(s, axis=-1, keepdims=True)
    e = jnp.exp(s - m)
    den = jnp.sum(e, axis=-1, keepdims=True)
    p = e / den
    lse = (m + jnp.log(den))[..., 0]
    o = jnp.einsum('nbhqk,nbkhe->nbqhe', p.astype(v.dtype), vw,
                   preferred_element_type=jnp.float32)
    return from_blocks(o), from_blocks(jnp.swapaxes(lse, 2, 3))


def dilated_attention(x, w_qkv, w_out):
    B, S, D = x.shape
    qkv = (x @ w_qkv).reshape(B, S, 3, N_ATTN_GROUPS, N_SLOTS, HEAD_DIM)
    slopes = alibi_slopes()
    outs, lses = [], []
    for g, (w, d) in enumerate(ATTN_PATTERNS):
        o, lse = strided_window_attention(qkv[:, :, 0, g], qkv[:, :, 1, g], qkv[:, :, 2, g],
                                          d, w // d, slopes[g])
        outs.append(o)
        lses.append(lse)
    wts = jax.nn.softmax(jnp.stack(lses, axis=0), axis=0)
    merged = jnp.einsum('gbsh,gbshe->bshe', wts, jnp.stack(outs, axis=0)).astype(x.dtype)
    return merged.reshape(B, S, GROUP_WIDTH) @ w_out


def sqrelu_mlp(x, w_up, w_down):
    h = jax.nn.relu(x @ w_up)
    return (h * h) @ w_down


def setup_inputs(seed: int = 0) -> dict:
    key = jax.random.key(seed)
    ks = jax.random.split(key, 14)
    f32 = jnp.float32
    nrm = lambda k, shape, s: jax.random.normal(k, shape, f32) * s
    x = jax.random.normal(ks[0], (BATCH, SEQ, D_MODEL), f32)
    pool_w_in = nrm(ks[1], (N_POOL_LAYERS, D_MODEL, D_MODEL), D_MODEL ** -0.5)
    pool_w_group = nrm(ks[2], (N_POOL_LAYERS, N_POOL_GROUPS, POOL_GROUP_DIM, POOL_GROUP_DIM), POOL_GROUP_DIM ** -0.5)
    pool_scale = 1.0 + nrm(ks[3], (N_POOL_LAYERS, D_MODEL), 0.1)
    pool_w_out = nrm(ks[4], (N_POOL_LAYERS, D_MODEL, D_MODEL), DEEPNORM_BETA * D_MODEL ** -0.5)
    attn_w_qkv = nrm(ks[5], (N_ATTN_LAYERS, D_MODEL, QKV_WIDTH), D_MODEL ** -0.5)
    attn_w_out = nrm(ks[6], (N_ATTN_LAYERS, GROUP_WIDTH, D_MODEL), DEEPNORM_BETA * GROUP_WIDTH ** -0.5)
    mlp_w_up = nrm(ks[7], (DEPTH, D_MODEL, D_FF), D_MODEL ** -0.5)
    mlp_w_down = nrm(ks[8], (DEPTH, D_FF, D_MODEL), DEEPNORM_BETA * D_FF ** -0.5)
    ln_mix_g = 1.0 + nrm(ks[9], (DEPTH, D_MODEL), 0.05)
    ln_mix_b = nrm(ks[10], (DEPTH, D_MODEL), 0.02)
    ln_mlp_g = 1.0 + nrm(ks[11], (DEPTH, D_MODEL), 0.05)
    ln_mlp_b = nrm(ks[12], (DEPTH, D_MODEL), 0.02)
    return {"x": x, "pool_w_in": pool_w_in, "pool_w_group": pool_w_group, "pool_scale": pool_scale,
            "pool_w_out": pool_w_out, "attn_w_qkv": attn_w_qkv, "attn_w_out": attn_w_out,
            "mlp_w_up": mlp_w_up, "mlp_w_down": mlp_w_down, "ln_mix_g": ln_mix_g, "ln_mix_b": ln_mix_b,
            "ln_mlp_g": ln_mlp_g, "ln_mlp_b": ln_mlp_b}


def reference(x, pool_w_in, pool_w_group, pool_scale, pool_w_out, attn_w_qkv, attn_w_out,
              mlp_w_up, mlp_w_down, ln_mix_g, ln_mix_b, ln_mlp_g, ln_mlp_b):
    for i in range(DEPTH):
        j = i // N_MIXERS
        if i % N_MIXERS == 0:
            h = pool_mixer(x, pool_w_in[j], pool_w_group[j], pool_scale[j], pool_w_out[j])
        else:
            h = dilated_attention(x, attn_w_qkv[j], attn_w_out[j])
        x = layer_norm(DEEPNORM_ALPHA * x + h, ln_mix_g[i], ln_mix_b[i])
        x = layer_norm(DEEPNORM_ALPHA * x + sqrelu_mlp(x, mlp_w_up[i], mlp_w_down[i]),
                       ln_mlp_g[i], ln_mlp_b[i])
    return x
```

```python
import numpy as np
from contextlib import ExitStack
import concourse.bass as bass
import concourse.mybir as mybir
from concourse.bass_utils import run_bass_kernel_spmd

F32 = mybir.dt.float32
BF16 = mybir.dt.bfloat16
AF = mybir.ActivationFunctionType
ALU = mybir.AluOpType
AX = mybir.AxisListType
KDS = 8
SELF_WAIT = True

D = 2048
KD = 16
TB = 1024
NS = 8
SEQ = 2048
DFF = 8192
NB = 4
ALPHA = float(4 ** 0.25)
LN_EPS = 1e-5
ARENA_BYTES = 207 * 1024
DILS = (1, 4, 16)
NBLK = (16, 4, 1)
SLOPES = [[float(2.0 ** (-8.0 * (8 * g + j + 1) / 24.0)) for j in range(8)] for g in range(3)]
QSCALE = float(1.0 / np.sqrt(128.0))
NEG = -30000.0


LAST_COUNTS = {}


class Res:
    __slots__ = ("w", "r")

    def __init__(self):
        self.w = None
        self.r = {}


class Sched:
    def __init__(self, nc, st):
        self.nc = nc
        self.names = ("pe", "act", "dve", "pool", "sp")
        self.sem = {e: st.enter_context(nc.semaphore("s_" + e)) for e in self.names}
        self.dsems = {q: [st.enter_context(nc.semaphore("d_%s%d" % (q, i))) for i in range(KDS)]
                      for q in ("sp", "act", "pool")}
        self.reset()

    def reset(self):
        self.cnt = {e: 0 for e in self.names}
        self.prog = {e: [] for e in self.names}
        self.waited = {e: {} for e in self.names}
        self.dcnt = {q: 0 for q in self.dsems}
        self.dry = False

    def _collect(self, reads, writes):
        toks = []
        for b in reads:
            if b.w is not None:
                toks.append(b.w)
        for b in writes:
            if b.w is not None:
                toks.append(b.w)
            toks.extend(b.r.values())
        return toks

    def _waits(self, e, toks):
        need = {}
        own = self.sem[e]
        for (s, v) in toks:
            if s is own and (e == "pe" or not SELF_WAIT):
                continue
            key = id(s)
            if self.waited[e].get(key, 0) >= v:
                continue
            if need.get(key, (None, 0))[1] < v:
                need[key] = (s, v)
        for key, (s, v) in need.items():
            self.waited[e][key] = v
        return list(need.values())

    def _update(self, tok, reads, writes):
        for b in writes:
            b.w = tok
            b.r = {}
        for b in reads:
            if b not in writes:
                b.r[id(tok[0])] = tok

    def op(self, e, fn, reads=(), writes=()):
        if self.dry:
            return None
        waits = self._waits(e, self._collect(reads, writes))
        self.cnt[e] += 1
        tok = (self.sem[e], self.cnt[e])
        self.prog[e].append((waits, fn, True))
        self._update(tok, reads, writes)
        return tok

    def dma(self, q, out, in_, reads=(), writes=(), **kw):
        if self.dry:
            return None
        i = self.dcnt[q]
        self.dcnt[q] += 1
        s = self.dsems[q][i % KDS]
        v = 16 * (i // KDS + 1)
        toks = self._collect(reads, writes)
        if v > 16:
            toks.append((s, v - 16))
        waits = self._waits(q, toks)
        self.prog[q].append((waits, lambda eng: eng.dma_start(out=out, in_=in_, **kw).then_inc(s, 16), False))
        tok = (s, v)
        self._update(tok, reads, writes)
        return tok

    def barrier(self, e, toks):
        if self.dry:
            return
        self.prog[e].append((self._waits(e, [t for t in toks if t is not None]), None, False))

    def emit(self, block):
        decs = {"pe": block.tensor, "act": block.scalar, "dve": block.vector, "pool": block.gpsimd,
                "sp": block.sync}
        for e in self.names:
            prog = self.prog[e]
            sem = self.sem[e]

            def body(eng, prog=prog, sem=sem):
                for waits, fn, inc in prog:
                    for (s, v) in waits:
                        eng.wait_ge(s, v)
                    if fn is None:
                        continue
                    ins = fn(eng)
                    if inc:
                        ins.then_inc(sem, 1)
            decs[e](body)


class Arena:
    def __init__(self, t, nbytes):
        self.t = t
        self.off = 0
        self.nbytes = nbytes

    def alloc(self, dtype, *free):
        n = int(np.prod(free))
        esz = 2 if dtype is BF16 else 4
        nb = (n * esz + 63) // 64 * 64
        a = self.off
        self.off += nb
        assert self.off <= self.nbytes, ("arena overflow", self.off, self.nbytes)
        v = self.t[:, a // 4:(a + nb) // 4]
        if dtype is BF16:
            v = v.bitcast(BF16)
        v = v[:, 0:n]
        if len(free) == 2:
            v = v.rearrange("p (a b) -> p a b", a=free[0])
        elif len(free) == 3:
            v = v.rearrange("p (a b c) -> p a b c", a=free[0], b=free[1])
        return v


def build_program(stop=None, dbg=False):
    nc = bass.Bass("TRN2", target_bir_lowering=False)

    def din(name, shape):
        return nc.dram_tensor(name, list(shape), F32, kind="ExternalInput").ap()

    x = din("x", (4096, D))
    pool_w_in = din("pool_w_in", (1, D, D))
    pool_w_group = din("pool_w_group", (1, 4, 512, 512))
    pool_scale = din("pool_scale", (1, D))
    pool_w_out = din("pool_w_out", (1, D, D))
    attn_w_qkv = din("attn_w_qkv", (1, D, 9216))
    attn_w_out = din("attn_w_out", (1, 1024, D))
    mlp_w_up = din("mlp_w_up", (2, D, DFF))
    mlp_w_down = din("mlp_w_down", (2, DFF, D))
    ln_mix_g = din("ln_mix_g", (2, D))
    ln_mix_b = din("ln_mix_b", (2, D))
    ln_mlp_g = din("ln_mlp_g", (2, D))
    ln_mlp_b = din("ln_mlp_b", (2, D))
    out = nc.dram_tensor("out", [4096, D], F32, kind="ExternalOutput").ap()
    kind = dict(kind="ExternalOutput") if dbg else {}
    x2_d = nc.dram_tensor("x2_d", [4096, D], F32, **kind).ap()
    x2T_d = nc.dram_tensor("x2T_d", [2, 128, KD, SEQ], BF16).ap()
    og_d = nc.dram_tensor("og_d", [2, 3, SEQ, 132], F32).ap()
    mgT_d = nc.dram_tensor("mgT_d", [2, 128, 8, SEQ], BF16, **kind).ap()

    with ExitStack() as st:
        arena_t = st.enter_context(nc.sbuf_tensor("arena", [128, ARENA_BYTES // 4], F32))
        ps = st.enter_context(nc.psum_tensor("ps", [128, 8, 512], F32))
        S = Sched(nc, st)
        psb = [ps[:, b, :].bitcast(BF16) for b in range(8)]
        specs = []
        out_toks = []

        def record():
            A = Arena(arena_t, ARENA_BYTES)
            WR = [A.alloc(BF16, 8192) for _ in range(NB)]
            R_wr = [[Res(), Res(), Res()] for _ in range(NB)]
            R_ps = [Res() for _ in range(8)]
            wstate = {"i": 0, "issued": 0}

            def wissue(upto):
                while wstate["issued"] < min(upto, len(specs)):
                    p = wstate["issued"]
                    src, shape = specs[p]
                    n = int(np.prod(shape))
                    dst = WR[p % NB][:, 0:n]
                    if len(shape) == 2:
                        dst = dst.rearrange("p (a b) -> p a b", a=shape[0])
                    else:
                        dst = dst.rearrange("p (a b c) -> p a b c", a=shape[0], b=shape[1])
                    if isinstance(src, list):
                        for gi, sp in enumerate(src):
                            S.dma("pool", dst[:, :, gi, :], sp, writes=[R_wr[p % NB][gi]])
                    else:
                        S.dma("pool", dst, src, writes=R_wr[p % NB])
                    wstate["issued"] += 1

            def wnext(src, shape, lag=0):
                i = wstate["i"]
                wstate["i"] += 1
                if S.dry:
                    specs.append((src, shape))
                else:
                    wissue(i + NB - lag)
                n = int(np.prod(shape))
                v = WR[i % NB][:, 0:n]
                if len(shape) == 2:
                    v = v.rearrange("p (a b) -> p a b", a=shape[0])
                else:
                    v = v.rearrange("p (a b c) -> p a b c", a=shape[0], b=shape[1])
                return v, R_wr[i % NB]

            ident = A.alloc(BF16, 128)
            identf = A.alloc(F32, 128)
            negD = A.alloc(F32, 256)
            Mk = A.alloc(F32, 256)
            ctmp = A.alloc(F32, 256)
            invc = A.alloc(F32, 4, 16)
            epst = A.alloc(F32, 1)
            scol = A.alloc(F32, 16)
            hist = A.alloc(F32, 16, 16)
            FZ = A.alloc(F32, 1)
            R_c = Res()
            R_hist = [Res() for _ in range(16)]
            R_x2d = [Res() for _ in range(4)]
            R_x2Td = [Res(), Res()]
            R_mgTd = [Res(), Res()]

            def fence(res):
                tok = S.op("pool", lambda e: e.memset(FZ, 0.0), writes=res)
                for en in S.names:
                    S.barrier(en, [tok])
            S.op("pool", lambda e: e.iota(identf, [[1, 128]], base=0, channel_multiplier=-1,
                                          allow_small_or_imprecise_dtypes=True), writes=[R_c])
            S.op("dve", lambda e: e.tensor_scalar(ident, identf, 0.0, None, ALU.is_equal), reads=[R_c], writes=[R_c])
            S.op("pool", lambda e: e.iota(negD, [[1, 256]], base=-128, channel_multiplier=-1,
                                          allow_small_or_imprecise_dtypes=True), writes=[R_c])
            S.op("dve", lambda e: e.tensor_scalar(Mk, negD, 0.0, None, ALU.is_le), reads=[R_c], writes=[R_c])
            S.op("dve", lambda e: e.tensor_scalar(ctmp, negD, -128.0, None, ALU.is_ge), reads=[R_c], writes=[R_c])
            S.op("dve", lambda e: e.tensor_tensor(Mk, Mk, ctmp, ALU.mult), reads=[R_c], writes=[R_c])
            S.op("dve", lambda e: e.tensor_scalar(Mk, Mk, -1.0, -NEG, ALU.add, ALU.mult), reads=[R_c], writes=[R_c])
            S.op("pool", lambda e: e.iota(ctmp[:, 0:16], [[1, 16]], base=1, channel_multiplier=0,
                                          allow_small_or_imprecise_dtypes=True), reads=[R_c], writes=[R_c])
            for g in range(4):
                S.op("dve", lambda e, g=g: e.tensor_scalar(invc[:, g, :], ctmp[:, 0:16], float(2 ** (g + 1)), None, ALU.min),
                     reads=[R_c], writes=[R_c])
                S.op("dve", lambda e, g=g: e.reciprocal(invc[:, g, :], invc[:, g, :]), reads=[R_c], writes=[R_c])
            S.op("dve", lambda e: e.memset(epst, LN_EPS), writes=[R_c])
            S.dma("sp", scol, pool_scale[0].rearrange("(j p) -> p j", p=128), writes=[R_c],
                  allow_slow_non_contiguous=True)

            region0 = A.off

            XRES = A.alloc(F32, NS, D)
            XT = A.alloc(BF16, KD, TB)
            HT = [A.alloc(BF16, 4, TB) for _ in range(2)]
            GT = A.alloc(F32, D)
            BT = A.alloc(F32, D)
            XB = [A.alloc(BF16, D) for _ in range(2)]
            STT = A.alloc(F32, 8, NS)
            BS = [A.alloc(F32, 4, 6) for _ in range(2)]
            STS = [A.alloc(F32, 8) for _ in range(2)]
            R_bs = [Res(), Res()]
            regionAC_end = A.off
            A.off = region0 + NS * D * 4 + KD * TB * 2 + 2 * 4 * TB * 2
            UT = [A.alloc(F32, 528) for _ in range(2)]
            SA = [A.alloc(F32, 528) for _ in range(2)]
            SB = [A.alloc(F32, 528) for _ in range(2)]
            T16 = A.alloc(F32, 16)
            assert A.off <= regionAC_end
            A.off = region0 + NS * D * 4 + KD * TB * 2 + 2 * 4 * TB * 2 + D * 4
            RT = [A.alloc(F32, 512) for _ in range(2)]
            A.off = regionAC_end
            R_u = [Res(), Res()]
            R_xres = [[Res() for _ in range(4)] for _ in range(NS)]
            R_xt = [Res() for _ in range(NS)]
            R_ht = [[[Res() for _ in range(2)] for _ in range(4)] for _ in range(2)]
            R_gb = Res()
            R_xb = [Res(), Res()]
            R_rt = [Res(), Res()]
            R_st = Res()
            cnt = {"a": 0, "t": 0, "xb": 0, "rt": 0, "u": 0, "ev": 0, "bs": 0}

            def ev_engine():
                cnt["ev"] += 1
                return "act" if cnt["ev"] % 2 else "dve"

            def copy_on(e, o, i):
                if e == "act":
                    return lambda eng: eng.activation(o, i, AF.Copy)
                return lambda eng: eng.tensor_copy(o, i)

            def transposes(src_bf, R_src, s):
                for half in range(2):
                    bank = 2 + cnt["t"] % 2
                    cnt["t"] += 1
                    for kk in range(8):
                        k = half * 8 + kk
                        S.op("pe", lambda e, k=k, kk=kk, bank=bank: e.transpose(
                            psb[bank][:, kk * 128:(kk + 1) * 128], src_bf[:, k * 128:(k + 1) * 128], ident),
                            reads=[R_src, R_c], writes=[R_ps[bank]])
                    eng = ev_engine()
                    S.op(eng, copy_on(eng, XT[:, half * 8:(half + 1) * 8, s * 128:(s + 1) * 128],
                                      psb[bank].rearrange("p (a b) -> p a b", a=8)),
                         reads=[R_ps[bank]], writes=[R_xt[s]])

            def cast_and_transpose(s):
                i = cnt["xb"] % 2
                cnt["xb"] += 1
                S.op("act", lambda e, i=i, s=s: e.activation(XB[i], XRES[:, s, :], AF.Copy),
                     reads=R_xres[s], writes=[R_xb[i]])
                transposes(XB[i], R_xb[i], s)

            LN_LAG = 1

            class LN:
                def __init__(self, g_ap, b_ap, post):
                    self.g_ap, self.b_ap, self.post = g_ap, b_ap, post
                    self.xb = {}

                def pre(self):
                    S.dma("sp", GT, self.g_ap.partition_broadcast(128), writes=[R_gb])
                    S.dma("sp", BT, self.b_ap.partition_broadcast(128), writes=[R_gb])

                def a(self, s):
                    bi = cnt["bs"] % 2
                    cnt["bs"] += 1
                    bs, st_, R_b = BS[bi], STS[bi], R_bs[bi]
                    for c4 in range(4):
                        S.op("dve", lambda e, c4=c4: e.bn_stats(bs[:, c4, :], XRES[:, s, c4 * 512:(c4 + 1) * 512]),
                             reads=[R_xres[s][c4]], writes=[R_b])
                    S.op("dve", lambda e: e.bn_aggr(st_[:, 0:2], bs.rearrange("p a b -> p (a b)")), reads=[R_b], writes=[R_b])
                    S.op("act", lambda e: e.activation(st_[:, 2:3], st_[:, 1:2], AF.Sqrt, bias=epst[:, 0:1]),
                         reads=[R_b, R_c], writes=[R_b])
                    S.op("dve", lambda e: e.reciprocal(st_[:, 3:4], st_[:, 2:3]), reads=[R_b], writes=[R_b])
                    S.op("dve", lambda e: e.scalar_tensor_tensor(st_[:, 4:5], st_[:, 0:1], -1.0, st_[:, 3:4],
                                                                 ALU.mult, ALU.mult), reads=[R_b], writes=[R_b])
                    S.op("act", lambda e: e.activation(XRES[:, s, :], XRES[:, s, :], AF.Identity,
                                                       scale=st_[:, 3:4], bias=st_[:, 4:5]),
                         reads=[R_b] + R_xres[s], writes=R_xres[s])
                    S.op("dve", lambda e: e.tensor_tensor(XRES[:, s, :], XRES[:, s, :], GT, ALU.mult),
                         reads=[R_gb] + R_xres[s], writes=R_xres[s])
                    S.op("pool", lambda e: e.tensor_tensor(XRES[:, s, :], XRES[:, s, :], BT, ALU.add),
                         reads=[R_gb] + R_xres[s], writes=R_xres[s])
                    i = cnt["xb"] % 2
                    cnt["xb"] += 1
                    S.op("act", lambda e, i=i: e.activation(XB[i], XRES[:, s, :], AF.Copy),
                         reads=R_xres[s], writes=[R_xb[i]])
                    self.xb[s] = i

                def b(self, s):
                    i = self.xb[s]
                    transposes(XB[i], R_xb[i], s)
                    self.post(s)

                def after(self, s):
                    self.a(s)
                    if s >= LN_LAG:
                        self.b(s - LN_LAG)
                    if s == NS - 1:
                        for t in range(NS - LN_LAG, NS):
                            self.b(t)

            def acc_evac(s, nb, first):
                o = XRES[:, s, nb * 512:(nb + 1) * 512]
                if first:
                    S.op("dve", lambda e: e.scalar_tensor_tensor(o, o, ALPHA, ps[:, 4 + nb, :], ALU.mult, ALU.add),
                         reads=[R_ps[4 + nb], R_xres[s][nb]], writes=[R_xres[s][nb]])
                else:
                    S.op("dve", lambda e: e.tensor_tensor(o, o, ps[:, 4 + nb, :], ALU.add),
                         reads=[R_ps[4 + nb], R_xres[s][nb]], writes=[R_xres[s][nb]])

            def down_proj(hbuf, R_h, wd, R_wd, nk, first, wsel=None, ln=None):
                for s in range(NS):
                    for nb in range(4):
                        for kk in range(nk):
                            w_ap, R_w = (wd[:, kk, nb * 512:(nb + 1) * 512], R_wd) if wsel is None else wsel(kk, nb)
                            S.op("pe", lambda e, s=s, nb=nb, kk=kk, w_ap=w_ap: e.matmul(
                                ps[:, 4 + nb, :], hbuf[:, kk, s * 128:(s + 1) * 128], w_ap,
                                start=(kk == 0), stop=(kk == nk - 1)),
                                reads=[R_h(kk, s)] + R_w, writes=[R_ps[4 + nb]])
                        acc_evac(s, nb, first)
                    if ln is not None:
                        ln.after(s)

            def up_tile(wv, R_w, jj, half, nk, rhs_of, R_rhs):
                bank = cnt["a"] % 2
                cnt["a"] += 1
                for k in range(nk):
                    S.op("pe", lambda e, k=k, bank=bank: e.matmul(
                        ps[:, bank, :], wv[:, k, jj * 128:(jj + 1) * 128], rhs_of(k, half),
                        start=(k == 0), stop=(k == nk - 1)),
                        reads=R_w + R_rhs(half), writes=[R_ps[bank]])
                return bank

            def xt_rhs(k, half):
                return XT[:, k, half * 512:(half + 1) * 512]

            def R_xt_half(half):
                return R_xt[4 * half:4 * half + 4]

            def mlp(layer, ln):
                wu_all = mlp_w_up[layer].rearrange("(k p) c -> p k c", p=128)
                wd_all = mlp_w_down[layer].rearrange("(k p) c -> p k c", p=128)
                for f in range(DFF // 512):
                    wu, R_wu = wnext(wu_all[:, :, f * 512:(f + 1) * 512], (KD, 512))
                    hb = f % 2
                    for half in range(2):
                        for jj in range(4):
                            bank = up_tile(wu, R_wu, jj, half, KD, xt_rhs, R_xt_half)
                            ri = cnt["rt"] % 2
                            cnt["rt"] += 1
                            S.op("act", lambda e, bank=bank, ri=ri: e.activation(RT[ri], ps[:, bank, :], AF.Relu),
                                 reads=[R_ps[bank]], writes=[R_rt[ri], R_gb])
                            S.op("dve", lambda e, ri=ri, jj=jj, half=half, hb=hb: e.tensor_tensor(
                                HT[hb][:, jj, half * 512:(half + 1) * 512], RT[ri], RT[ri], ALU.mult),
                                reads=[R_rt[ri], R_gb], writes=[R_ht[hb][jj][half]])
                    wd, R_wd = wnext(wd_all[:, 4 * f:4 * f + 4, :], (4, D))
                    last = (f == DFF // 512 - 1)
                    if last:
                        ln.pre()
                    down_proj(HT[hb], lambda kk, s, hb=hb: R_ht[hb][kk][s // 4], wd, R_wd, 4, first=(f == 0),
                              ln=ln if last else None)

            def pool_mixer(seq_start, ln):
                win_all = pool_w_in[0].rearrange("(k p) c -> p k c", p=128)
                wout_all = pool_w_out[0].rearrange("(k p) c -> p k c", p=128)
                for g in range(4):
                    w = 2 ** (g + 1)
                    win, R_win = wnext(win_all[:, :, g * 512:(g + 1) * 512], (KD, 512))
                    PT, R_pt = HT[0], R_ht[0]
                    Y2, R_y2 = HT[1], R_ht[1]
                    for half in range(2):
                        for jj in range(4):
                            j = 4 * g + jj
                            bank = up_tile(win, R_win, jj, half, KD, xt_rhs, R_xt_half)
                            ui = cnt["u"] % 2
                            cnt["u"] += 1
                            U, Sa, Sb = UT[ui], SA[ui], SB[ui]
                            Ru = R_u[ui]
                            S.op("act", lambda e, U=U, bank=bank: e.activation(U[:, 16:528], ps[:, bank, :], AF.Copy),
                                 reads=[R_ps[bank]], writes=[Ru, R_gb])
                            if seq_start and half == 0:
                                S.op("pool", lambda e, U=U: e.memset(U[:, 0:16], 0.0), reads=[R_hist[j]], writes=[Ru])
                            else:
                                S.op("pool", lambda e, U=U, j=j: e.tensor_copy(U[:, 0:16], hist[:, j, :]),
                                     reads=[R_hist[j]], writes=[Ru])
                            S.op("pool", lambda e, U=U, j=j: e.tensor_copy(hist[:, j, :], U[:, 512:528]),
                                 reads=[Ru], writes=[R_hist[j]])
                            S.op("dve", lambda e, U=U, Sa=Sa: e.tensor_tensor(Sa[:, 1:528], U[:, 1:528], U[:, 0:527], ALU.add),
                                 reads=[Ru], writes=[Ru])
                            fin = Sa
                            if g >= 1:
                                S.op("dve", lambda e, Sa=Sa, Sb=Sb: e.tensor_tensor(Sb[:, 3:528], Sa[:, 3:528], Sa[:, 1:526], ALU.add),
                                     reads=[Ru], writes=[Ru])
                                fin = Sb
                            if g >= 2:
                                S.op("dve", lambda e, Sa=Sa, Sb=Sb: e.tensor_tensor(Sa[:, 7:528], Sb[:, 7:528], Sb[:, 3:524], ALU.add),
                                     reads=[Ru], writes=[Ru])
                                fin = Sa
                            if g >= 3:
                                S.op("dve", lambda e, Sa=Sa, Sb=Sb: e.tensor_tensor(Sb[:, 15:528], Sa[:, 15:528], Sa[:, 7:520], ALU.add),
                                     reads=[Ru], writes=[Ru])
                                fin = Sb
                            pdst = PT[:, jj, half * 512:(half + 1) * 512]
                            S.op("dve", lambda e, fin=fin, U=U, pdst=pdst, w=w: e.scalar_tensor_tensor(
                                pdst, fin[:, 16:528], 1.0 / w, U[:, 16:528], ALU.mult, ALU.subtract),
                                reads=[Ru, R_gb], writes=[R_pt[jj][half]])
                            if seq_start and half == 0:
                                S.op("dve", lambda e, fin=fin, g=g: e.tensor_tensor(T16, fin[:, 16:32], invc[:, g, :], ALU.mult),
                                     reads=[Ru, R_c], writes=[R_st])
                                S.op("dve", lambda e, U=U, jj=jj: e.tensor_tensor(PT[:, jj, 0:16], T16, U[:, 16:32], ALU.subtract),
                                     reads=[Ru, R_st, R_gb], writes=[R_pt[jj][half]])
                    wg, R_wg = wnext(pool_w_group[0, g].rearrange("(k p) c -> p k c", p=128), (4, 512))
                    for half in range(2):
                        for jj in range(4):
                            j = 4 * g + jj
                            bank = up_tile(wg, R_wg, jj, half, 4, lambda k, half: PT[:, k, half * 512:(half + 1) * 512],
                                           lambda half: [R_pt[k][half] for k in range(4)])
                            S.op("act", lambda e, bank=bank, jj=jj, half=half, j=j: e.activation(
                                Y2[:, jj, half * 512:(half + 1) * 512], ps[:, bank, :], AF.Identity, scale=scol[:, j:j + 1]),
                                reads=[R_ps[bank], R_c], writes=[R_y2[jj][half]])
                    wo, R_wo = wnext(wout_all[:, 4 * g:4 * g + 4, :], (4, D))
                    if g == 3:
                        ln.pre()
                    down_proj(Y2, lambda kk, s: R_y2[kk][s // 4], wo, R_wo, 4, first=(g == 0), ln=ln if g == 3 else None)

            def attn_out(seq, hb, ln):
                S.dma("sp", XT[:, 0:8, :], mgT_d[seq, :, :, hb * TB:(hb + 1) * TB], reads=[R_mgTd[seq]], writes=R_xt)
                wo_all = attn_w_out[0].rearrange("(k p) c -> p k c", p=128)
                w0, R_w0 = wnext(wo_all[:, :, 0:1024], (8, 1024))
                w1, R_w1 = wnext(wo_all[:, :, 1024:2048], (8, 1024), lag=1)

                def wsel(kk, nb):
                    wv, R_w = (w0, R_w0) if nb < 2 else (w1, R_w1)
                    return wv[:, kk, (nb % 2) * 512:(nb % 2 + 1) * 512], R_w
                ln.pre()
                down_proj(XT, lambda kk, s: R_xt[s], None, None, 8, first=True, wsel=wsel, ln=ln)

            def load_block(src, tok0, rd=()):
                S.dma("sp", XRES, src[tok0:tok0 + TB, :].rearrange("(s p) d -> p s d", p=128),
                      reads=list(rd), writes=[r for rs in R_xres for r in rs])

            def phase_A(blk):
                tok0 = blk * TB
                seq, hb = blk // 2, blk % 2
                load_block(x, tok0)
                for s in range(NS):
                    cast_and_transpose(s)
                pool_mixer((hb == 0), LN(ln_mix_g[0], ln_mix_b[0], lambda s: None))

                def post(s):
                    S.dma("sp", x2_d[tok0 + s * 128: tok0 + (s + 1) * 128, :], XRES[:, s, :], reads=R_xres[s], writes=[R_x2d[blk]])
                    if s == NS - 1:
                        S.dma("sp", x2T_d[seq, :, :, hb * TB:(hb + 1) * TB], XT, reads=R_xt, writes=[R_x2Td[seq]])
                mlp(0, LN(ln_mlp_g[0], ln_mlp_b[0], post))

            def phase_C(blk):
                tok0 = blk * TB
                seq, hb = blk // 2, blk % 2
                load_block(x2_d, tok0, [R_x2d[blk]])
                attn_out(seq, hb, LN(ln_mix_g[1], ln_mix_b[1], lambda s: None))

                def post(s):
                    out_toks.append(S.dma("sp", out[tok0 + s * 128: tok0 + (s + 1) * 128, :], XRES[:, s, :],
                                          reads=R_xres[s]))
                mlp(1, LN(ln_mlp_g[1], ln_mlp_b[1], post))

            def phase_B(seq):
                A.off = region0
                X2T = A.alloc(BF16, KD, SEQ)
                QT = [A.alloc(BF16, SEQ) for _ in range(3)]
                KT = [A.alloc(BF16, SEQ) for _ in range(3)]
                VT = [A.alloc(BF16, SEQ) for _ in range(2)]
                VTOK = A.alloc(BF16, 3, 16, 128)
                BIAS = [A.alloc(F32, 256) for _ in range(3)]
                NSB, NR, NO, NM = 3, 5, 9, 4
                SK2, SK3 = 3, 6
                SBt = [A.alloc(F32, 256) for _ in range(NSB)]
                PEX = [A.alloc(BF16, 256) for _ in range(NR)]
                PTS = [A.alloc(BF16, 2, 128) for _ in range(NR)]
                OST = [A.alloc(F32, 132) for _ in range(NO)]
                MG = [A.alloc(F32, 3, 132) for _ in range(NM)]
                SM = [A.alloc(F32, 16) for _ in range(NM)]
                ACC = [A.alloc(F32, 128) for _ in range(NM)]
                MB = [A.alloc(BF16, 128) for _ in range(NM)]
                MGTS = [A.alloc(BF16, TB) for _ in range(2)]
                R_all_ac = [r for rs in R_xres for r in rs] + R_xt + [r for a in R_ht for b in a for r in b] + \
                    [R_gb, R_st] + R_xb + R_rt + R_bs + R_u
                R_x2t = Res()
                R_q = [Res() for _ in range(3)]
                R_k = [Res() for _ in range(3)]
                R_vt = [Res(), Res()]
                R_vtok = [[Res() for _ in range(16)] for _ in range(3)]
                R_bias = [Res() for _ in range(3)]
                R_sb = [Res() for _ in range(NSB)]
                R_pex = [Res() for _ in range(NR)]
                R_pts = [Res() for _ in range(NR)]
                R_ost = [Res() for _ in range(NO)]
                R_mg = [Res() for _ in range(NM)]
                R_sm = [Res() for _ in range(NM)]
                R_acc = [Res() for _ in range(NM)]
                R_mb = [Res() for _ in range(NM)]
                R_mgts = [Res(), Res()]
                R_og = [[[Res() for _ in range(16)] for _ in range(3)] for _ in range(2)]
                fence(R_all_ac)
                S.dma("sp", X2T, x2T_d[seq], reads=[R_x2Td[seq]], writes=[R_x2t])
                wq_all = attn_w_qkv[0].rearrange("(k p) (c g h e) -> p k c g h e", p=128, c=3, g=3, h=8)
                psS = [ps[:, 6, 0:256], ps[:, 7, 0:256], ps[:, 6, 256:512], ps[:, 7, 256:512]]
                c = {"vt": 0, "u": 0, "pt": 0, "po": 0, "mg": 0, "mt": 0}
                wcur = {}

                def proj_tiles(j, cc, g):
                    def first():
                        if cc not in wcur or wcur[cc][0] != j:
                            wv, R_w = wnext([wq_all[:, :, cc, gi, j, :] for gi in range(3)], (KD, 3, 128))
                            wcur[cc] = (j, wv, R_w)
                        if cc == 2:
                            wcur["vi"] = c["vt"] % 2
                            c["vt"] += 1

                    def tile(tq):
                        if tq == 0:
                            first()
                        _, wv, R_w = wcur[cc]
                        d = DILS[g]
                        if cc == 2:
                            vi = wcur["vi"]
                            dstT, R_dst = VT[vi], R_vt[vi]
                        else:
                            dstT, R_dst = (QT[g], R_q[g]) if cc == 0 else (KT[g], R_k[g])
                        bank = cnt["a"] % 2
                        cnt["a"] += 1
                        for k in range(KD):
                            S.op("pe", lambda e, k=k: e.matmul(
                                ps[:, bank, :], wv[:, k, g, :], X2T[:, k, tq * 512:(tq + 1) * 512],
                                start=(k == 0), stop=(k == KD - 1)),
                                reads=R_w + [R_x2t], writes=[R_ps[bank]])
                        n = 512 // d
                        dst = dstT.rearrange("p (r i) -> p r i", r=d)[:, :, tq * n:(tq + 1) * n]
                        src = ps[:, bank, :].rearrange("p (i r) -> p r i", r=d)
                        eng = ev_engine()
                        if cc == 0:
                            if eng == "act":
                                S.op("act", lambda e: e.activation(dst, src, AF.Identity, scale=QSCALE),
                                     reads=[R_ps[bank]], writes=[R_dst])
                            else:
                                S.op("dve", lambda e: e.tensor_scalar(dst, src, QSCALE, None, ALU.mult),
                                     reads=[R_ps[bank]], writes=[R_dst])
                        else:
                            S.op(eng, copy_on(eng, dst, src), reads=[R_ps[bank]], writes=[R_dst])
                        if cc == 2 and tq == 3:
                            for u0 in (0, 8):
                                for uu in range(8):
                                    u = u0 + uu
                                    S.op("pe", lambda e, u=u, uu=uu: e.transpose(
                                        psb[5][:, uu * 128:(uu + 1) * 128], dstT[:, u * 128:(u + 1) * 128], ident),
                                        reads=[R_dst, R_c], writes=[R_ps[5]])
                                eng = ev_engine()
                                S.op(eng, copy_on(eng, VTOK[:, g, u0:u0 + 8, :],
                                                  psb[5].rearrange("p (a b) -> p a b", a=8)),
                                     reads=[R_ps[5]], writes=R_vtok[g][u0:u0 + 8])
                    return [lambda tq=tq: tile(tq) for tq in range(4)]

                def unit_s1(j, g, u):
                    b = u % NBLK[g]
                    nk = 256 if b > 0 else 128
                    i = c["u"] % NR
                    bi = c["u"] % NSB
                    si = c["u"] % 4
                    oi = c["u"] % NO
                    c["u"] += 1
                    k0 = (u - 1) * 128 if b > 0 else u * 128
                    bank = 6 + si % 2
                    S.op("pe", lambda e: e.matmul(psS[si][:, 0:nk], QT[g][:, u * 128:(u + 1) * 128],
                                                  KT[g][:, k0:k0 + nk], start=True, stop=True),
                         reads=[R_q[g], R_k[g]], writes=[R_ps[bank]])
                    bsl = BIAS[g][:, 0:256] if b > 0 else BIAS[g][:, 128:256]
                    S.op("dve", lambda e: e.tensor_tensor(SBt[bi][:, 0:nk], psS[si][:, 0:nk], bsl, ALU.add),
                         reads=[R_ps[bank], R_bias[g]], writes=[R_sb[bi]])
                    S.op("dve", lambda e: e.tensor_reduce(OST[oi][:, 129:130], SBt[bi][:, 0:nk], AX.X, ALU.max, negate=True),
                         reads=[R_sb[bi]], writes=[R_ost[oi]])
                    S.op("act", lambda e: e.activation(PEX[i][:, 0:nk], SBt[bi][:, 0:nk], AF.Exp, bias=OST[oi][:, 129:130],
                                                       accum_out=OST[oi][:, 128:129]),
                         reads=[R_sb[bi], R_ost[oi]], writes=[R_pex[i], R_ost[oi]])
                    return dict(j=j, g=g, u=u, b=b, nk=nk, i=i, oi=oi, par=j % 2)

                def unit_s2(t):
                    i, nk = t["i"], t["nk"]
                    nt = nk // 128
                    q4 = c["pt"] % 4
                    c["pt"] += 1
                    pv = psb[2][:, q4 * 256:(q4 + 1) * 256]
                    for cc in range(nt):
                        S.op("pe", lambda e, cc=cc: e.transpose(pv[:, cc * 128:(cc + 1) * 128],
                                                               PEX[i][:, cc * 128:(cc + 1) * 128], ident),
                             reads=[R_pex[i], R_c], writes=[R_ps[2]])
                    S.op("act", copy_on("act", PTS[i][:, 0:nt, :], pv[:, 0:nk].rearrange("p (a b) -> p a b", a=nt)),
                         reads=[R_ps[2]], writes=[R_pts[i]])

                def unit_s3(t):
                    i, nk, g, u, b, oi, par = t["i"], t["nk"], t["g"], t["u"], t["b"], t["oi"], t["par"]
                    nt = nk // 128
                    q4 = c["po"] % 4
                    c["po"] += 1
                    po = ps[:, 3, q4 * 128:(q4 + 1) * 128]
                    for cc in range(nt):
                        vu = (u - 1 + cc) if b > 0 else u
                        S.op("pe", lambda e, cc=cc, vu=vu: e.matmul(po, PTS[i][:, cc, :], VTOK[:, g, vu, :],
                                                                  start=(cc == 0), stop=(cc == nt - 1)),
                             reads=[R_pts[i], R_vtok[g][vu]], writes=[R_ps[3]])
                    S.op("act", lambda e: e.activation(OST[oi][:, 0:128], po, AF.Copy),
                         reads=[R_ps[3], R_ost[oi]], writes=[R_ost[oi]])
                    d = DILS[g]
                    r = u // NBLK[g]
                    dst = og_d[par, g].rearrange("(i r) c -> r i c", r=d)[r, 128 * b:128 * b + 128, 0:130]
                    S.dma("sp", dst, OST[oi][:, 0:130], reads=[R_ost[oi]], writes=[R_og[par][g][u]])

                pipe = []
                pst = {"n": 0}

                def pipe_step(unit):
                    t = pst["n"]
                    pst["n"] += 1
                    pipe.append(unit_s1(*unit) if unit is not None else None)
                    if t >= SK2 and pipe[t - SK2] is not None:
                        unit_s2(pipe[t - SK2])
                    if t >= SK3 and pipe[t - SK3] is not None:
                        unit_s3(pipe[t - SK3])

                def pipe_flush():
                    for _ in range(SK3):
                        pipe_step(None)

                mslot = {}

                def m_load(j, par, n):
                    all_og = [R_og[par][g][u] for g in range(3) for u in range(16)]
                    mi = c["mg"] % NM
                    c["mg"] += 1
                    mslot[(j, n)] = mi
                    S.dma("sp", MG[mi], og_d[par, :, n * 128:(n + 1) * 128, :].rearrange("g t c -> t g c"),
                          reads=all_og, writes=[R_mg[mi]])

                def m_a(j, n):
                    mi = mslot[(j, n)]
                    mg, sm = MG[mi], SM[mi]
                    S.op("act", lambda e: e.activation(sm[:, 0:3], mg[:, :, 128], AF.Ln),
                         reads=[R_mg[mi]], writes=[R_sm[mi]])
                    S.op("dve", lambda e: e.tensor_tensor(sm[:, 0:3], sm[:, 0:3], mg[:, :, 129], ALU.subtract),
                         reads=[R_mg[mi], R_sm[mi]], writes=[R_sm[mi]])
                    S.op("dve", lambda e: e.tensor_reduce(sm[:, 3:4], sm[:, 0:3], AX.X, ALU.max, negate=True),
                         reads=[R_sm[mi]], writes=[R_sm[mi]])
                    S.op("act", lambda e: e.activation(sm[:, 4:7], sm[:, 0:3], AF.Exp, bias=sm[:, 3:4],
                                                       accum_out=sm[:, 7:8]),
                         reads=[R_sm[mi]], writes=[R_sm[mi]])
                    S.op("dve", lambda e: e.reciprocal(sm[:, 8:11], mg[:, :, 128]),
                         reads=[R_mg[mi], R_sm[mi]], writes=[R_sm[mi]])
                    S.op("dve", lambda e: e.reciprocal(sm[:, 11:12], sm[:, 7:8]),
                         reads=[R_sm[mi]], writes=[R_sm[mi]])
                    S.op("dve", lambda e: e.tensor_tensor(sm[:, 12:15], sm[:, 4:7], sm[:, 8:11], ALU.mult),
                         reads=[R_sm[mi]], writes=[R_sm[mi]])
                    S.op("dve", lambda e: e.tensor_scalar(sm[:, 12:15], sm[:, 12:15], sm[:, 11:12], None, ALU.mult),
                         reads=[R_sm[mi]], writes=[R_sm[mi]])
                    acc, mb = ACC[mi], MB[mi]
                    S.op("act", lambda e: e.activation(acc, mg[:, 0, 0:128], AF.Identity, scale=sm[:, 12:13]),
                         reads=[R_mg[mi], R_sm[mi]], writes=[R_acc[mi]])
                    S.op("dve", lambda e: e.scalar_tensor_tensor(acc, mg[:, 1, 0:128], sm[:, 13:14], acc, ALU.mult, ALU.add),
                         reads=[R_mg[mi], R_sm[mi], R_acc[mi]], writes=[R_acc[mi]])
                    S.op("dve", lambda e: e.scalar_tensor_tensor(mb, mg[:, 2, 0:128], sm[:, 14:15], acc, ALU.mult, ALU.add),
                         reads=[R_mg[mi], R_sm[mi], R_acc[mi]], writes=[R_mb[mi]])

                def m_b(j, n):
                    mi = mslot[(j, n)]
                    mb = MB[mi]
                    nn = n % 8
                    S.op("pe", lambda e: e.transpose(psb[4][:, nn * 128:(nn + 1) * 128], mb, ident),
                         reads=[R_mb[mi], R_c], writes=[R_ps[4]])
                    if nn == 7:
                        ti = c["mt"] % 2
                        c["mt"] += 1
                        eng = ev_engine()
                        S.op(eng, copy_on(eng, MGTS[ti], psb[4]), reads=[R_ps[4]], writes=[R_mgts[ti]])
                        hb = n // 8
                        S.dma("sp", mgT_d[seq, :, j, hb * TB:(hb + 1) * TB], MGTS[ti], reads=[R_mgts[ti]],
                              writes=[R_mgTd[seq]])

                def merge_steps(j):
                    par = j % 2

                    def step(k):
                        if k == 0:
                            m_load(j, par, 0)
                            m_load(j, par, 1)
                        if k + 2 < 16:
                            m_load(j, par, k + 2)
                        if k < 16:
                            m_a(j, k)
                        if 0 <= k - 2 < 16:
                            m_b(j, k - 2)
                    return [lambda k=k: step(k) for k in range(18)]

                def set_bias(j, g):
                    a = SLOPES[g][j] * DILS[g]
                    S.op("dve", lambda e: e.scalar_tensor_tensor(BIAS[g], negD, a, Mk, ALU.mult, ALU.add),
                         reads=[R_c, R_bias[g]], writes=[R_bias[g]])

                def interleave(tiles, extras):
                    nt_, ne = len(tiles), len(extras)
                    done = 0
                    for ti, tl in enumerate(tiles):
                        tl()
                        want = (ne * (ti + 1)) // nt_ if nt_ else ne
                        while done < want:
                            extras[done]()
                            done += 1
                    while done < ne:
                        extras[done]()
                        done += 1

                def unit_steps(j, g):
                    return [lambda u=u: pipe_step((j, g, u)) for u in range(16)]

                for j in range(8):
                    t1 = proj_tiles(j, 1, 0) + proj_tiles(j, 1, 1)
                    if j > 0:
                        set_bias(j - 1, 2)
                        interleave(t1, unit_steps(j - 1, 2) + [pipe_flush])
                    else:
                        interleave(t1, [])
                    t2 = proj_tiles(j, 1, 2) + proj_tiles(j, 2, 0) + proj_tiles(j, 2, 1) + proj_tiles(j, 2, 2) + \
                        proj_tiles(j, 0, 0)
                    ex = merge_steps(j - 1) if j > 0 else []
                    interleave(t2, ex)
                    set_bias(j, 0)
                    interleave(proj_tiles(j, 0, 1), unit_steps(j, 0))
                    set_bias(j, 1)
                    interleave(proj_tiles(j, 0, 2), unit_steps(j, 1))
                set_bias(7, 2)
                interleave([], unit_steps(7, 2) + [pipe_flush])
                for st_ in merge_steps(7):
                    st_()
                fence_l = [R_x2t] + R_q + R_k + R_vt + [r for a in R_vtok for r in a] + R_bias + R_sb + R_pex + R_pts + \
                    R_ost + R_mg + R_sm + R_acc + R_mb + R_mgts
                fence(fence_l + R_all_ac)

            for seq in range(2):
                phase_A(2 * seq)
                if stop == "A1":
                    break
                phase_A(2 * seq + 1)
                if stop == "A":
                    break
                phase_B(seq)
                if stop == "B":
                    break
                phase_C(2 * seq)
                phase_C(2 * seq + 1)
                if stop == "C":
                    break
            S.barrier("sp", out_toks)

        S.dry = True
        record()
        S.reset()
        del out_toks[:]
        record()
        final = []
        for q in S.dsems:
            n = S.dcnt[q]
            for i, s in enumerate(S.dsems[q]):
                if n > i:
                    final.append((s, 16 * ((n - 1 - i) // KDS + 1)))
        S.barrier("sp", final)
        LAST_COUNTS.update(S.cnt)
        LAST_COUNTS.update({'d_' + q: n for q, n in S.dcnt.items()})
        with nc.Block() as block:
            S.emit(block)
    return nc


_NAMES = ["pool_w_in", "pool_w_group", "pool_scale", "pool_w_out", "attn_w_qkv", "attn_w_out",
          "mlp_w_up", "mlp_w_down", "ln_mix_g", "ln_mix_b", "ln_mlp_g", "ln_mlp_b"]


def kernel(**inputs):
    x = np.ascontiguousarray(np.asarray(inputs["x"], dtype=np.float32))
    n = 8
    xs = x.reshape(n, 4096, D)
    shared = {k: np.ascontiguousarray(np.asarray(inputs[k], dtype=np.float32)) for k in _NAMES}
    nc = build_program()
    in_maps = []
    for c in range(n):
        m = {"x": xs[c]}
        m.update(shared)
        in_maps.append(m)
    res = run_bass_kernel_spmd(nc, in_maps, core_ids=list(range(n)))
    outs = [np.asarray(r["out"], dtype=np.float32).reshape(2, SEQ, D) for r in res.results]
    return np.concatenate(outs, axis=0)
```

```python
import numpy as np
from contextlib import ExitStack
import concourse.bass as bass
import concourse.mybir as mybir
from concourse.bass_utils import run_bass_kernel_spmd

F32 = mybir.dt.float32
BF16 = mybir.dt.bfloat16
AF = mybir.ActivationFunctionType
ALU = mybir.AluOpType
AX = mybir.AxisListType
KDS = 8
SELF_WAIT = True

D = 2048
KD = 16
TB = 1024
NS = 8
SEQ = 2048
DFF = 8192
NB = 4
ALPHA = float(4 ** 0.25)
LN_EPS = 1e-5
ARENA_BYTES = 207 * 1024
DILS = (1, 4, 16)
NBLK = (16, 4, 1)
SLOPES = [[float(2.0 ** (-8.0 * (8 * g + j + 1) / 24.0)) for j in range(8)] for g in range(3)]
QSCALE = float(1.0 / np.sqrt(128.0))
NEG = -30000.0


LAST_COUNTS = {}


class Res:
    __slots__ = ("w", "r")

    def __init__(self):
        self.w = None
        self.r = {}


class Sched:
    def __init__(self, nc, st):
        self.nc = nc
        self.names = ("pe", "act", "dve", "pool", "sp")
        self.sem = {e: st.enter_context(nc.semaphore("s_" + e)) for e in self.names}
        self.dsems = {q: [st.enter_context(nc.semaphore("d_%s%d" % (q, i))) for i in range(KDS)]
                      for q in ("sp", "act", "pool")}
        self.reset()

    def reset(self):
        self.cnt = {e: 0 for e in self.names}
        self.prog = {e: [] for e in self.names}
        self.waited = {e: {} for e in self.names}
        self.dcnt = {q: 0 for q in self.dsems}
        self.dry = False

    def _collect(self, reads, writes):
        toks = []
        for b in reads:
            if b.w is not None:
                toks.append(b.w)
        for b in writes:
            if b.w is not None:
                toks.append(b.w)
            toks.extend(b.r.values())
        return toks

    def _waits(self, e, toks):
        need = {}
        own = self.sem[e]
        for (s, v) in toks:
            if s is own and (e == "pe" or not SELF_WAIT):
                continue
            key = id(s)
            if self.waited[e].get(key, 0) >= v:
                continue
            if need.get(key, (None, 0))[1] < v:
                need[key] = (s, v)
        for key, (s, v) in need.items():
            self.waited[e][key] = v
        return list(need.values())

    def _update(self, tok, reads, writes):
        for b in writes:
            b.w = tok
            b.r = {}
        for b in reads:
            if b not in writes:
                b.r[id(tok[0])] = tok

    def op(self, e, fn, reads=(), writes=()):
        if self.dry:
            return None
        waits = self._waits(e, self._collect(reads, writes))
        self.cnt[e] += 1
        tok = (self.sem[e], self.cnt[e])
        self.prog[e].append((waits, fn, True))
        self._update(tok, reads, writes)
        return tok

    def dma(self, q, out, in_, reads=(), writes=(), **kw):
        if self.dry:
            return None
        i = self.dcnt[q]
        self.dcnt[q] += 1
        s = self.dsems[q][i % KDS]
        v = 16 * (i // KDS + 1)
        toks = self._collect(reads, writes)
        if v > 16:
            toks.append((s, v - 16))
        waits = self._waits(q, toks)
        self.prog[q].append((waits, lambda eng: eng.dma_start(out=out, in_=in_, **kw).then_inc(s, 16), False))
        tok = (s, v)
        self._update(tok, reads, writes)
        return tok

    def barrier(self, e, toks):
        if self.dry:
            return
        self.prog[e].append((self._waits(e, [t for t in toks if t is not None]), None, False))

    def emit(self, block):
        decs = {"pe": block.tensor, "act": block.scalar, "dve": block.vector, "pool": block.gpsimd,
                "sp": block.sync}
        for e in self.names:
            prog = self.prog[e]
            sem = self.sem[e]

            def body(eng, prog=prog, sem=sem):
                for waits, fn, inc in prog:
                    for (s, v) in waits:
                        eng.wait_ge(s, v)
                    if fn is None:
                        continue
                    ins = fn(eng)
                    if inc:
                        ins.then_inc(sem, 1)
            decs[e](body)


class Arena:
    def __init__(self, t, nbytes):
        self.t = t
        self.off = 0
        self.nbytes = nbytes

    def alloc(self, dtype, *free):
        n = int(np.prod(free))
        esz = 2 if dtype is BF16 else 4
        nb = (n * esz + 63) // 64 * 64
        a = self.off
        self.off += nb
        assert self.off <= self.nbytes, ("arena overflow", self.off, self.nbytes)
        v = self.t[:, a // 4:(a + nb) // 4]
        if dtype is BF16:
            v = v.bitcast(BF16)
        v = v[:, 0:n]
        if len(free) == 2:
            v = v.rearrange("p (a b) -> p a b", a=free[0])
        elif len(free) == 3:
            v = v.rearrange("p (a b c) -> p a b c", a=free[0], b=free[1])
        return v


def build_program(stop=None, dbg=False):
    nc = bass.Bass("TRN2", target_bir_lowering=False)

    def din(name, shape):
        return nc.dram_tensor(name, list(shape), F32, kind="ExternalInput").ap()

    x = din("x", (4096, D))
    pool_w_in = din("pool_w_in", (1, D, D))
    pool_w_group = din("pool_w_group", (1, 4, 512, 512))
    pool_scale = din("pool_scale", (1, D))
    pool_w_out = din("pool_w_out", (1, D, D))
    attn_w_qkv = din("attn_w_qkv", (1, D, 9216))
    attn_w_out = din("attn_w_out", (1, 1024, D))
    mlp_w_up = din("mlp_w_up", (2, D, DFF))
    mlp_w_down = din("mlp_w_down", (2, DFF, D))
    ln_mix_g = din("ln_mix_g", (2, D))
    ln_mix_b = din("ln_mix_b", (2, D))
    ln_mlp_g = din("ln_mlp_g", (2, D))
    ln_mlp_b = din("ln_mlp_b", (2, D))
    out = nc.dram_tensor("out", [4096, D], F32, kind="ExternalOutput").ap()
    kind = dict(kind="ExternalOutput") if dbg else {}
    x2_d = nc.dram_tensor("x2_d", [4096, D], F32, **kind).ap()
    x2T_d = nc.dram_tensor("x2T_d", [2, 128, KD, SEQ], BF16).ap()
    og_d = nc.dram_tensor("og_d", [2, 3, SEQ, 132], F32).ap()
    mgT_d = nc.dram_tensor("mgT_d", [2, 128, 8, SEQ], BF16, **kind).ap()

    with ExitStack() as st:
        arena_t = st.enter_context(nc.sbuf_tensor("arena", [128, ARENA_BYTES // 4], F32))
        ps = st.enter_context(nc.psum_tensor("ps", [128, 8, 512], F32))
        S = Sched(nc, st)
        psb = [ps[:, b, :].bitcast(BF16) for b in range(8)]
        specs = []
        out_toks = []

        def record():
            A = Arena(arena_t, ARENA_BYTES)
            WR = [A.alloc(BF16, 8192) for _ in range(NB)]
            R_wr = [[Res(), Res(), Res()] for _ in range(NB)]
            R_ps = [Res() for _ in range(8)]
            wstate = {"i": 0, "issued": 0}

            def wissue(upto):
                while wstate["issued"] < min(upto, len(specs)):
                    p = wstate["issued"]
                    src, shape = specs[p]
                    n = int(np.prod(shape))
                    dst = WR[p % NB][:, 0:n]
                    if len(shape) == 2:
                        dst = dst.rearrange("p (a b) -> p a b", a=shape[0])
                    else:
                        dst = dst.rearrange("p (a b c) -> p a b c", a=shape[0], b=shape[1])
                    if isinstance(src, list):
                        for gi, sp in enumerate(src):
                            S.dma("pool", dst[:, :, gi, :], sp, writes=[R_wr[p % NB][gi]])
                    else:
                        S.dma("pool", dst, src, writes=R_wr[p % NB])
                    wstate["issued"] += 1

            def wnext(src, shape, lag=0):
                i = wstate["i"]
                wstate["i"] += 1
                if S.dry:
                    specs.append((src, shape))
                else:
                    wissue(i + NB - lag)
                n = int(np.prod(shape))
                v = WR[i % NB][:, 0:n]
                if len(shape) == 2:
                    v = v.rearrange("p (a b) -> p a b", a=shape[0])
                else:
                    v = v.rearrange("p (a b c) -> p a b c", a=shape[0], b=shape[1])
                return v, R_wr[i % NB]

            ident = A.alloc(BF16, 128)
            identf = A.alloc(F32, 128)
            negD = A.alloc(F32, 256)
            Mk = A.alloc(F32, 256)
            ctmp = A.alloc(F32, 256)
            invc = A.alloc(F32, 4, 16)
            epst = A.alloc(F32, 1)
            scol = A.alloc(F32, 16)
            hist = A.alloc(F32, 16, 16)
            FZ = A.alloc(F32, 1)
            R_c = Res()
            R_hist = [Res() for _ in range(16)]
            R_x2d = [Res() for _ in range(4)]
            R_x2Td = [Res(), Res()]
            R_mgTd = [Res(), Res()]

            def fence(res):
                tok = S.op("pool", lambda e: e.memset(FZ, 0.0), writes=res)
                for en in S.names:
                    S.barrier(en, [tok])
            S.op("pool", lambda e: e.iota(identf, [[1, 128]], base=0, channel_multiplier=-1,
                                          allow_small_or_imprecise_dtypes=True), writes=[R_c])
            S.op("dve", lambda e: e.tensor_scalar(ident, identf, 0.0, None, ALU.is_equal), reads=[R_c], writes=[R_c])
            S.op("pool", lambda e: e.iota(negD, [[1, 256]], base=-128, channel_multiplier=-1,
                                          allow_small_or_imprecise_dtypes=True), writes=[R_c])
            S.op("dve", lambda e: e.tensor_scalar(Mk, negD, 0.0, None, ALU.is_le), reads=[R_c], writes=[R_c])
            S.op("dve", lambda e: e.tensor_scalar(ctmp, negD, -128.0, None, ALU.is_ge), reads=[R_c], writes=[R_c])
            S.op("dve", lambda e: e.tensor_tensor(Mk, Mk, ctmp, ALU.mult), reads=[R_c], writes=[R_c])
            S.op("dve", lambda e: e.tensor_scalar(Mk, Mk, -1.0, -NEG, ALU.add, ALU.mult), reads=[R_c], writes=[R_c])
            S.op("pool", lambda e: e.iota(ctmp[:, 0:16], [[1, 16]], base=1, channel_multiplier=0,
                                          allow_small_or_imprecise_dtypes=True), reads=[R_c], writes=[R_c])
            for g in range(4):
                S.op("dve", lambda e, g=g: e.tensor_scalar(invc[:, g, :], ctmp[:, 0:16], float(2 ** (g + 1)), None, ALU.min),
                     reads=[R_c], writes=[R_c])
                S.op("dve", lambda e, g=g: e.reciprocal(invc[:, g, :], invc[:, g, :]), reads=[R_c], writes=[R_c])
            S.op("dve", lambda e: e.memset(epst, LN_EPS), writes=[R_c])
            S.dma("sp", scol, pool_scale[0].rearrange("(j p) -> p j", p=128), writes=[R_c],
                  allow_slow_non_contiguous=True)

            region0 = A.off

            XRES = A.alloc(F32, NS, D)
            XT = A.alloc(BF16, KD, TB)
            HT = [A.alloc(BF16, 4, TB) for _ in range(2)]
            GT = A.alloc(F32, D)
            BT = A.alloc(F32, D)
            XB = [A.alloc(BF16, D) for _ in range(2)]
            STT = A.alloc(F32, 8, NS)
            BS = [A.alloc(F32, 4, 6) for _ in range(2)]
            STS = [A.alloc(F32, 8) for _ in range(2)]
            R_bs = [Res(), Res()]
            regionAC_end = A.off
            A.off = region0 + NS * D * 4 + KD * TB * 2 + 2 * 4 * TB * 2
            UT = [A.alloc(F32, 528) for _ in range(2)]
            SA = [A.alloc(F32, 528) for _ in range(2)]
            SB = [A.alloc(F32, 528) for _ in range(2)]
            T16 = A.alloc(F32, 16)
            assert A.off <= regionAC_end
            A.off = region0 + NS * D * 4 + KD * TB * 2 + 2 * 4 * TB * 2 + D * 4
            RT = [A.alloc(F32, 512) for _ in range(2)]
            A.off = regionAC_end
            R_u = [Res(), Res()]
            R_xres = [[Res() for _ in range(4)] for _ in range(NS)]
            R_xt = [Res() for _ in range(NS)]
            R_ht = [[[Res() for _ in range(2)] for _ in range(4)] for _ in range(2)]
            R_gb = Res()
            R_bt = Res()
            R_xb = [Res(), Res()]
            R_rt = [Res(), Res()]
            R_st = Res()
            cnt = {"a": 0, "t": 0, "xb": 0, "rt": 0, "u": 0, "ev": 0, "bs": 0, "xb4": 0}

            def ev_engine():
                cnt["ev"] += 1
                return "act" if cnt["ev"] % 2 else "dve"

            def copy_on(e, o, i):
                if e == "act":
                    return lambda eng: eng.activation(o, i, AF.Copy)
                return lambda eng: eng.tensor_copy(o, i)

            def transposes(src_bf, R_src, s):
                for half in range(2):
                    bank = 2 + cnt["t"] % 2
                    cnt["t"] += 1
                    for kk in range(8):
                        k = half * 8 + kk
                        S.op("pe", lambda e, k=k, kk=kk, bank=bank: e.transpose(
                            psb[bank][:, kk * 128:(kk + 1) * 128], src_bf[:, k * 128:(k + 1) * 128], ident),
                            reads=(R_src if isinstance(R_src, list) else [R_src]) + [R_c], writes=[R_ps[bank]])
                    eng = ev_engine()
                    S.op(eng, copy_on(eng, XT[:, half * 8:(half + 1) * 8, s * 128:(s + 1) * 128],
                                      psb[bank].rearrange("p (a b) -> p a b", a=8)),
                         reads=[R_ps[bank]], writes=[R_xt[s]])

            def cast_and_transpose(s):
                i = cnt["xb"] % 2
                cnt["xb"] += 1
                S.op("act", lambda e, i=i, s=s: e.activation(XB[i], XRES[:, s, :], AF.Copy),
                     reads=R_xres[s], writes=[R_xb[i]])
                transposes(XB[i], R_xb[i], s)

            LN_LAG = 3
            XB4 = [XB[0], XB[1], HT[0][:, 0:2, :].rearrange("p a b -> p (a b)"), HT[0][:, 2:4, :].rearrange("p a b -> p (a b)")]
            R_xb4 = [[R_xb[0]], [R_xb[1]], [R_ht[0][0][0], R_ht[0][0][1], R_ht[0][1][0], R_ht[0][1][1]],
                     [R_ht[0][2][0], R_ht[0][2][1], R_ht[0][3][0], R_ht[0][3][1]]]

            class LN:
                def __init__(self, g_ap, b_ap, post, need_xt=True):
                    self.g_ap, self.b_ap, self.post, self.need_xt = g_ap, b_ap, post, need_xt
                    self.xb = {}

                def pre(self):
                    S.dma("sp", GT, self.g_ap.partition_broadcast(128), writes=[R_gb])
                    S.dma("sp", BT, self.b_ap.partition_broadcast(128), writes=[R_bt])

                def a(self, s):
                    bi = cnt["bs"] % 2
                    cnt["bs"] += 1
                    bs, st_, R_b = BS[bi], STS[bi], R_bs[bi]
                    for c4 in range(4):
                        S.op("dve", lambda e, c4=c4: e.bn_stats(bs[:, c4, :], XRES[:, s, c4 * 512:(c4 + 1) * 512]),
                             reads=[R_xres[s][c4]], writes=[R_b])
                    S.op("dve", lambda e: e.bn_aggr(st_[:, 0:2], bs.rearrange("p a b -> p (a b)")), reads=[R_b], writes=[R_b])
                    S.op("act", lambda e: e.activation(st_[:, 2:3], st_[:, 1:2], AF.Sqrt, bias=epst[:, 0:1]),
                         reads=[R_b, R_c], writes=[R_b])
                    S.op("dve", lambda e: e.reciprocal(st_[:, 3:4], st_[:, 2:3]), reads=[R_b], writes=[R_b])
                    S.op("dve", lambda e: e.scalar_tensor_tensor(st_[:, 4:5], st_[:, 0:1], -1.0, st_[:, 3:4],
                                                                 ALU.mult, ALU.mult), reads=[R_b], writes=[R_b])
                    S.op("act", lambda e: e.activation(XRES[:, s, :], XRES[:, s, :], AF.Identity,
                                                       scale=st_[:, 3:4], bias=st_[:, 4:5]),
                         reads=[R_b] + R_xres[s], writes=R_xres[s])
                    S.op("dve", lambda e: e.tensor_tensor(XRES[:, s, :], XRES[:, s, :], GT, ALU.mult),
                         reads=[R_gb] + R_xres[s], writes=R_xres[s])
                    S.op("pool", lambda e: e.tensor_tensor(XRES[:, s, :], XRES[:, s, :], BT, ALU.add),
                         reads=[R_bt] + R_xres[s], writes=R_xres[s])
                    if not self.need_xt:
                        return
                    i = cnt["xb4"] % 4
                    cnt["xb4"] += 1
                    S.op("act", lambda e, i=i: e.activation(XB4[i], XRES[:, s, :], AF.Copy),
                         reads=R_xres[s], writes=R_xb4[i])
                    self.xb[s] = i

                def b(self, s):
                    if self.need_xt:
                        i = self.xb[s]
                        transposes(XB4[i], R_xb4[i], s)
                    self.post(s)

                def after(self, s):
                    self.a(s)
                    if s >= LN_LAG:
                        self.b(s - LN_LAG)
                    if s == NS - 1:
                        for t in range(NS - LN_LAG, NS):
                            self.b(t)

            def acc_evac(s, nb, first):
                o = XRES[:, s, nb * 512:(nb + 1) * 512]
                if first:
                    S.op("dve", lambda e: e.scalar_tensor_tensor(o, o, ALPHA, ps[:, 4 + nb, :], ALU.mult, ALU.add),
                         reads=[R_ps[4 + nb], R_xres[s][nb]], writes=[R_xres[s][nb]])
                else:
                    S.op("dve", lambda e: e.tensor_tensor(o, o, ps[:, 4 + nb, :], ALU.add),
                         reads=[R_ps[4 + nb], R_xres[s][nb]], writes=[R_xres[s][nb]])

            def down_proj(hbuf, R_h, wd, R_wd, nk, first, wsel=None, ln=None):
                for s in range(NS):
                    for nb in range(4):
                        for kk in range(nk):
                            w_ap, R_w = (wd[:, kk, nb * 512:(nb + 1) * 512], R_wd) if wsel is None else wsel(kk, nb)
                            S.op("pe", lambda e, s=s, nb=nb, kk=kk, w_ap=w_ap: e.matmul(
                                ps[:, 4 + nb, :], hbuf[:, kk, s * 128:(s + 1) * 128], w_ap,
                                start=(kk == 0), stop=(kk == nk - 1)),
                                reads=[R_h(kk, s)] + R_w, writes=[R_ps[4 + nb]])
                        acc_evac(s, nb, first)
                    if ln is not None:
                        ln.after(s)

            def up_tile(wv, R_w, jj, half, nk, rhs_of, R_rhs):
                bank = cnt["a"] % 2
                cnt["a"] += 1
                for k in range(nk):
                    S.op("pe", lambda e, k=k, bank=bank: e.matmul(
                        ps[:, bank, :], wv[:, k, jj * 128:(jj + 1) * 128], rhs_of(k, half),
                        start=(k == 0), stop=(k == nk - 1)),
                        reads=R_w + R_rhs(half), writes=[R_ps[bank]])
                return bank

            def xt_rhs(k, half):
                return XT[:, k, half * 512:(half + 1) * 512]

            def R_xt_half(half):
                return R_xt[4 * half:4 * half + 4]

            def mlp(layer, ln):
                wu_all = mlp_w_up[layer].rearrange("(k p) c -> p k c", p=128)
                wd_all = mlp_w_down[layer].rearrange("(k p) c -> p k c", p=128)
                for f in range(DFF // 512):
                    wu, R_wu = wnext(wu_all[:, :, f * 512:(f + 1) * 512], (KD, 512))
                    hb = f % 2
                    for half in range(2):
                        for jj in range(4):
                            bank = up_tile(wu, R_wu, jj, half, KD, xt_rhs, R_xt_half)
                            ri = cnt["rt"] % 2
                            cnt["rt"] += 1
                            S.op("act", lambda e, bank=bank, ri=ri: e.activation(RT[ri], ps[:, bank, :], AF.Relu),
                                 reads=[R_ps[bank]], writes=[R_rt[ri], R_bt])
                            S.op("dve", lambda e, ri=ri, jj=jj, half=half, hb=hb: e.tensor_tensor(
                                HT[hb][:, jj, half * 512:(half + 1) * 512], RT[ri], RT[ri], ALU.mult),
                                reads=[R_rt[ri], R_bt], writes=[R_ht[hb][jj][half]])
                    wd, R_wd = wnext(wd_all[:, 4 * f:4 * f + 4, :], (4, D))
                    last = (f == DFF // 512 - 1)
                    if last:
                        ln.pre()
                    down_proj(HT[hb], lambda kk, s, hb=hb: R_ht[hb][kk][s // 4], wd, R_wd, 4, first=(f == 0),
                              ln=ln if last else None)

            def pool_mixer(seq_start, ln):
                win_all = pool_w_in[0].rearrange("(k p) c -> p k c", p=128)
                wout_all = pool_w_out[0].rearrange("(k p) c -> p k c", p=128)
                for g in range(4):
                    w = 2 ** (g + 1)
                    win, R_win = wnext(win_all[:, :, g * 512:(g + 1) * 512], (KD, 512))
                    PT, R_pt = HT[0], R_ht[0]
                    Y2, R_y2 = HT[1], R_ht[1]
                    for half in range(2):
                        for jj in range(4):
                            j = 4 * g + jj
                            bank = up_tile(win, R_win, jj, half, KD, xt_rhs, R_xt_half)
                            ui = cnt["u"] % 2
                            cnt["u"] += 1
                            U, Sa, Sb = UT[ui], SA[ui], SB[ui]
                            Ru = R_u[ui]
                            S.op("act", lambda e, U=U, bank=bank: e.activation(U[:, 16:528], ps[:, bank, :], AF.Copy),
                                 reads=[R_ps[bank]], writes=[Ru, R_gb, R_bt])
                            if seq_start and half == 0:
                                S.op("pool", lambda e, U=U: e.memset(U[:, 0:16], 0.0), reads=[R_hist[j]], writes=[Ru])
                            else:
                                S.op("pool", lambda e, U=U, j=j: e.tensor_copy(U[:, 0:16], hist[:, j, :]),
                                     reads=[R_hist[j]], writes=[Ru])
                            S.op("pool", lambda e, U=U, j=j: e.tensor_copy(hist[:, j, :], U[:, 512:528]),
                                 reads=[Ru], writes=[R_hist[j]])
                            S.op("dve", lambda e, U=U, Sa=Sa: e.tensor_tensor(Sa[:, 1:528], U[:, 1:528], U[:, 0:527], ALU.add),
                                 reads=[Ru], writes=[Ru])
                            fin = Sa
                            if g >= 1:
                                S.op("dve", lambda e, Sa=Sa, Sb=Sb: e.tensor_tensor(Sb[:, 3:528], Sa[:, 3:528], Sa[:, 1:526], ALU.add),
                                     reads=[Ru], writes=[Ru])
                                fin = Sb
                            if g >= 2:
                                S.op("dve", lambda e, Sa=Sa, Sb=Sb: e.tensor_tensor(Sa[:, 7:528], Sb[:, 7:528], Sb[:, 3:524], ALU.add),
                                     reads=[Ru], writes=[Ru])
                                fin = Sa
                            if g >= 3:
                                S.op("dve", lambda e, Sa=Sa, Sb=Sb: e.tensor_tensor(Sb[:, 15:528], Sa[:, 15:528], Sa[:, 7:520], ALU.add),
                                     reads=[Ru], writes=[Ru])
                                fin = Sb
                            pdst = PT[:, jj, half * 512:(half + 1) * 512]
                            S.op("dve", lambda e, fin=fin, U=U, pdst=pdst, w=w: e.scalar_tensor_tensor(
                                pdst, fin[:, 16:528], 1.0 / w, U[:, 16:528], ALU.mult, ALU.subtract),
                                reads=[Ru, R_gb, R_bt], writes=[R_pt[jj][half]])
                            if seq_start and half == 0:
                                S.op("dve", lambda e, fin=fin, g=g: e.tensor_tensor(T16, fin[:, 16:32], invc[:, g, :], ALU.mult),
                                     reads=[Ru, R_c], writes=[R_st])
                                S.op("dve", lambda e, U=U, jj=jj: e.tensor_tensor(PT[:, jj, 0:16], T16, U[:, 16:32], ALU.subtract),
                                     reads=[Ru, R_st, R_gb, R_bt], writes=[R_pt[jj][half]])
                    wg, R_wg = wnext(pool_w_group[0, g].rearrange("(k p) c -> p k c", p=128), (4, 512))
                    for half in range(2):
                        for jj in range(4):
                            j = 4 * g + jj
                            bank = up_tile(wg, R_wg, jj, half, 4, lambda k, half: PT[:, k, half * 512:(half + 1) * 512],
                                           lambda half: [R_pt[k][half] for k in range(4)])
                            S.op("act", lambda e, bank=bank, jj=jj, half=half, j=j: e.activation(
                                Y2[:, jj, half * 512:(half + 1) * 512], ps[:, bank, :], AF.Identity, scale=scol[:, j:j + 1]),
                                reads=[R_ps[bank], R_c], writes=[R_y2[jj][half]])
                    wo, R_wo = wnext(wout_all[:, 4 * g:4 * g + 4, :], (4, D))
                    if g == 3:
                        ln.pre()
                    down_proj(Y2, lambda kk, s: R_y2[kk][s // 4], wo, R_wo, 4, first=(g == 0), ln=ln if g == 3 else None)

            def attn_out(seq, hb, ln):
                S.dma("sp", XT[:, 0:8, :], mgT_d[seq, :, :, hb * TB:(hb + 1) * TB], reads=[R_mgTd[seq]], writes=R_xt)
                wo_all = attn_w_out[0].rearrange("(k p) c -> p k c", p=128)
                w0, R_w0 = wnext(wo_all[:, :, 0:1024], (8, 1024))
                w1, R_w1 = wnext(wo_all[:, :, 1024:2048], (8, 1024), lag=1)

                def wsel(kk, nb):
                    wv, R_w = (w0, R_w0) if nb < 2 else (w1, R_w1)
                    return wv[:, kk, (nb % 2) * 512:(nb % 2 + 1) * 512], R_w
                ln.pre()
                down_proj(XT, lambda kk, s: R_xt[s], None, None, 8, first=True, wsel=wsel, ln=ln)

            def load_block(src, tok0, rd=()):
                for s in range(NS):
                    S.dma("sp", XRES[:, s, :], src[tok0 + s * 128:tok0 + (s + 1) * 128, :],
                          reads=list(rd), writes=R_xres[s])

            def phase_A(blk):
                tok0 = blk * TB
                seq, hb = blk // 2, blk % 2
                load_block(x, tok0)
                for s in range(NS):
                    cast_and_transpose(s)
                pool_mixer((hb == 0), LN(ln_mix_g[0], ln_mix_b[0], lambda s: None))

                def post(s):
                    S.dma("sp", x2_d[tok0 + s * 128: tok0 + (s + 1) * 128, :], XRES[:, s, :], reads=R_xres[s], writes=[R_x2d[blk]])
                    if s == NS - 1:
                        S.dma("sp", x2T_d[seq, :, :, hb * TB:(hb + 1) * TB], XT, reads=R_xt, writes=[R_x2Td[seq]])
                mlp(0, LN(ln_mlp_g[0], ln_mlp_b[0], post))

            def phase_C(blk):
                tok0 = blk * TB
                seq, hb = blk // 2, blk % 2
                load_block(x2_d, tok0, [R_x2d[blk]])
                attn_out(seq, hb, LN(ln_mix_g[1], ln_mix_b[1], lambda s: None))

                def post(s):
                    out_toks.append(S.dma("sp", out[tok0 + s * 128: tok0 + (s + 1) * 128, :], XRES[:, s, :],
                                          reads=R_xres[s]))
                mlp(1, LN(ln_mlp_g[1], ln_mlp_b[1], post, need_xt=False))

            def phase_B(seq):
                A.off = region0
                X2T = A.alloc(BF16, KD, SEQ)
                QT = [A.alloc(BF16, SEQ) for _ in range(3)]
                KT = [A.alloc(BF16, SEQ) for _ in range(3)]
                VT = [A.alloc(BF16, SEQ) for _ in range(2)]
                VTOK = A.alloc(BF16, 3, 16, 128)
                BIAS = [A.alloc(F32, 256) for _ in range(3)]
                NSB, NR, NO, NM = 3, 5, 9, 4
                SK2, SK3 = 3, 6
                SBt = [A.alloc(F32, 256) for _ in range(NSB)]
                PEX = [A.alloc(BF16, 256) for _ in range(NR)]
                PTS = [A.alloc(BF16, 2, 128) for _ in range(NR)]
                OST = [A.alloc(F32, 132) for _ in range(NO)]
                MG = [A.alloc(F32, 3, 132) for _ in range(NM)]
                SM = [A.alloc(F32, 16) for _ in range(NM)]
                ACC = [A.alloc(F32, 128) for _ in range(NM)]
                MB = [A.alloc(BF16, 128) for _ in range(NM)]
                MGTS = [A.alloc(BF16, TB) for _ in range(2)]
                R_all_ac = [r for rs in R_xres for r in rs] + R_xt + [r for a in R_ht for b in a for r in b] + \
                    [R_gb, R_bt, R_st] + R_xb + R_rt + R_bs + R_u
                R_x2t = Res()
                R_q = [Res() for _ in range(3)]
                R_k = [Res() for _ in range(3)]
                R_vt = [Res(), Res()]
                R_vtok = [[Res() for _ in range(16)] for _ in range(3)]
                R_bias = [Res() for _ in range(3)]
                R_sb = [Res() for _ in range(NSB)]
                R_pex = [Res() for _ in range(NR)]
                R_pts = [Res() for _ in range(NR)]
                R_ost = [Res() for _ in range(NO)]
                R_mg = [Res() for _ in range(NM)]
                R_sm = [Res() for _ in range(NM)]
                R_acc = [Res() for _ in range(NM)]
                R_mb = [Res() for _ in range(NM)]
                R_mgts = [Res(), Res()]
                R_og = [[[Res() for _ in range(16)] for _ in range(3)] for _ in range(2)]
                fence(R_all_ac)
                S.dma("sp", X2T, x2T_d[seq], reads=[R_x2Td[seq]], writes=[R_x2t])
                wq_all = attn_w_qkv[0].rearrange("(k p) (c g h e) -> p k c g h e", p=128, c=3, g=3, h=8)
                psS = [ps[:, 6, 0:256], ps[:, 7, 0:256], ps[:, 6, 256:512], ps[:, 7, 256:512]]
                c = {"vt": 0, "u": 0, "pt": 0, "po": 0, "mg": 0, "mt": 0}
                wcur = {}

                def proj_tiles(j, cc, g):
                    def first():
                        if cc not in wcur or wcur[cc][0] != j:
                            wv, R_w = wnext([wq_all[:, :, cc, gi, j, :] for gi in range(3)], (KD, 3, 128))
                            wcur[cc] = (j, wv, R_w)
                        if cc == 2:
                            wcur["vi"] = c["vt"] % 2
                            c["vt"] += 1

                    def tile(tq):
                        if tq == 0:
                            first()
                        _, wv, R_w = wcur[cc]
                        d = DILS[g]
                        if cc == 2:
                            vi = wcur["vi"]
                            dstT, R_dst = VT[vi], R_vt[vi]
                        else:
                            dstT, R_dst = (QT[g], R_q[g]) if cc == 0 else (KT[g], R_k[g])
                        bank = cnt["a"] % 2
                        cnt["a"] += 1
                        for k in range(KD):
                            S.op("pe", lambda e, k=k: e.matmul(
                                ps[:, bank, :], wv[:, k, g, :], X2T[:, k, tq * 512:(tq + 1) * 512],
                                start=(k == 0), stop=(k == KD - 1)),
                                reads=R_w + [R_x2t], writes=[R_ps[bank]])
                        n = 512 // d
                        dst = dstT.rearrange("p (r i) -> p r i", r=d)[:, :, tq * n:(tq + 1) * n]
                        src = ps[:, bank, :].rearrange("p (i r) -> p r i", r=d)
                        eng = ev_engine()
                        if cc == 0:
                            if eng == "act":
                                S.op("act", lambda e: e.activation(dst, src, AF.Identity, scale=QSCALE),
                                     reads=[R_ps[bank]], writes=[R_dst])
                            else:
                                S.op("dve", lambda e: e.tensor_scalar(dst, src, QSCALE, None, ALU.mult),
                                     reads=[R_ps[bank]], writes=[R_dst])
                        else:
                            S.op(eng, copy_on(eng, dst, src), reads=[R_ps[bank]], writes=[R_dst])
                        if cc == 2 and tq == 3:
                            for u0 in (0, 8):
                                for uu in range(8):
                                    u = u0 + uu
                                    S.op("pe", lambda e, u=u, uu=uu: e.transpose(
                                        psb[5][:, uu * 128:(uu + 1) * 128], dstT[:, u * 128:(u + 1) * 128], ident),
                                        reads=[R_dst, R_c], writes=[R_ps[5]])
                                eng = ev_engine()
                                S.op(eng, copy_on(eng, VTOK[:, g, u0:u0 + 8, :],
                                                  psb[5].rearrange("p (a b) -> p a b", a=8)),
                                     reads=[R_ps[5]], writes=R_vtok[g][u0:u0 + 8])
                    return [lambda tq=tq: tile(tq) for tq in range(4)]

                def unit_s1(j, g, u):
                    b = u % NBLK[g]
                    nk = 256 if b > 0 else 128
                    i = c["u"] % NR
                    bi = c["u"] % NSB
                    si = c["u"] % 4
                    oi = c["u"] % NO
                    c["u"] += 1
                    k0 = (u - 1) * 128 if b > 0 else u * 128
                    bank = 6 + si % 2
                    S.op("pe", lambda e: e.matmul(psS[si][:, 0:nk], QT[g][:, u * 128:(u + 1) * 128],
                                                  KT[g][:, k0:k0 + nk], start=True, stop=True),
                         reads=[R_q[g], R_k[g]], writes=[R_ps[bank]])
                    bsl = BIAS[g][:, 0:256] if b > 0 else BIAS[g][:, 128:256]
                    S.op("dve", lambda e: e.tensor_tensor(SBt[bi][:, 0:nk], psS[si][:, 0:nk], bsl, ALU.add),
                         reads=[R_ps[bank], R_bias[g]], writes=[R_sb[bi]])
                    S.op("dve", lambda e: e.tensor_reduce(OST[oi][:, 129:130], SBt[bi][:, 0:nk], AX.X, ALU.max, negate=True),
                         reads=[R_sb[bi]], writes=[R_ost[oi]])
                    S.op("act", lambda e: e.activation(PEX[i][:, 0:nk], SBt[bi][:, 0:nk], AF.Exp, bias=OST[oi][:, 129:130],
                                                       accum_out=OST[oi][:, 128:129]),
                         reads=[R_sb[bi], R_ost[oi]], writes=[R_pex[i], R_ost[oi]])
                    return dict(j=j, g=g, u=u, b=b, nk=nk, i=i, oi=oi, par=j % 2)

                def unit_s2(t):
                    i, nk = t["i"], t["nk"]
                    nt = nk // 128
                    q4 = c["pt"] % 4
                    c["pt"] += 1
                    pv = psb[2][:, q4 * 256:(q4 + 1) * 256]
                    for cc in range(nt):
                        S.op("pe", lambda e, cc=cc: e.transpose(pv[:, cc * 128:(cc + 1) * 128],
                                                               PEX[i][:, cc * 128:(cc + 1) * 128], ident),
                             reads=[R_pex[i], R_c], writes=[R_ps[2]])
                    S.op("act", copy_on("act", PTS[i][:, 0:nt, :], pv[:, 0:nk].rearrange("p (a b) -> p a b", a=nt)),
                         reads=[R_ps[2]], writes=[R_pts[i]])

                def unit_s3(t):
                    i, nk, g, u, b, oi, par = t["i"], t["nk"], t["g"], t["u"], t["b"], t["oi"], t["par"]
                    nt = nk // 128
                    q4 = c["po"] % 4
                    c["po"] += 1
                    po = ps[:, 3, q4 * 128:(q4 + 1) * 128]
                    for cc in range(nt):
                        vu = (u - 1 + cc) if b > 0 else u
                        S.op("pe", lambda e, cc=cc, vu=vu: e.matmul(po, PTS[i][:, cc, :], VTOK[:, g, vu, :],
                                                                  start=(cc == 0), stop=(cc == nt - 1)),
                             reads=[R_pts[i], R_vtok[g][vu]], writes=[R_ps[3]])
                    S.op("act", lambda e: e.activation(OST[oi][:, 0:128], po, AF.Copy),
                         reads=[R_ps[3], R_ost[oi]], writes=[R_ost[oi]])
                    d = DILS[g]
                    r = u // NBLK[g]
                    dst = og_d[par, g].rearrange("(i r) c -> r i c", r=d)[r, 128 * b:128 * b + 128, 0:130]
                    S.dma("sp", dst, OST[oi][:, 0:130], reads=[R_ost[oi]], writes=[R_og[par][g][u]])

                pipe = []
                pst = {"n": 0}

                def pipe_step(unit):
                    t = pst["n"]
                    pst["n"] += 1
                    pipe.append(unit_s1(*unit) if unit is not None else None)
                    if t >= SK2 and pipe[t - SK2] is not None:
                        unit_s2(pipe[t - SK2])
                    if t >= SK3 and pipe[t - SK3] is not None:
                        unit_s3(pipe[t - SK3])

                def pipe_flush():
                    for _ in range(SK3):
                        pipe_step(None)

                mslot = {}

                def m_load(j, par, n):
                    all_og = [R_og[par][g][u] for g in range(3) for u in range(16)]
                    mi = c["mg"] % NM
                    c["mg"] += 1
                    mslot[(j, n)] = mi
                    S.dma("sp", MG[mi], og_d[par, :, n * 128:(n + 1) * 128, :].rearrange("g t c -> t g c"),
                          reads=all_og, writes=[R_mg[mi]])

                def m_a(j, n):
                    mi = mslot[(j, n)]
                    mg, sm = MG[mi], SM[mi]
                    S.op("act", lambda e: e.activation(sm[:, 0:3], mg[:, :, 128], AF.Ln),
                         reads=[R_mg[mi]], writes=[R_sm[mi]])
                    S.op("dve", lambda e: e.tensor_tensor(sm[:, 0:3], sm[:, 0:3], mg[:, :, 129], ALU.subtract),
                         reads=[R_mg[mi], R_sm[mi]], writes=[R_sm[mi]])
                    S.op("dve", lambda e: e.tensor_reduce(sm[:, 3:4], sm[:, 0:3], AX.X, ALU.max, negate=True),
                         reads=[R_sm[mi]], writes=[R_sm[mi]])
                    S.op("act", lambda e: e.activation(sm[:, 4:7], sm[:, 0:3], AF.Exp, bias=sm[:, 3:4],
                                                       accum_out=sm[:, 7:8]),
                         reads=[R_sm[mi]], writes=[R_sm[mi]])
                    S.op("dve", lambda e: e.reciprocal(sm[:, 8:11], mg[:, :, 128]),
                         reads=[R_mg[mi], R_sm[mi]], writes=[R_sm[mi]])
                    S.op("dve", lambda e: e.reciprocal(sm[:, 11:12], sm[:, 7:8]),
                         reads=[R_sm[mi]], writes=[R_sm[mi]])
                    S.op("dve", lambda e: e.tensor_tensor(sm[:, 12:15], sm[:, 4:7], sm[:, 8:11], ALU.mult),
                         reads=[R_sm[mi]], writes=[R_sm[mi]])
                    S.op("dve", lambda e: e.tensor_scalar(sm[:, 12:15], sm[:, 12:15], sm[:, 11:12], None, ALU.mult),
                         reads=[R_sm[mi]], writes=[R_sm[mi]])
                    acc, mb = ACC[mi], MB[mi]
                    S.op("act", lambda e: e.activation(acc, mg[:, 0, 0:128], AF.Identity, scale=sm[:, 12:13]),
                         reads=[R_mg[mi], R_sm[mi]], writes=[R_acc[mi]])
                    S.op("dve", lambda e: e.scalar_tensor_tensor(acc, mg[:, 1, 0:128], sm[:, 13:14], acc, ALU.mult, ALU.add),
                         reads=[R_mg[mi], R_sm[mi], R_acc[mi]], writes=[R_acc[mi]])
                    S.op("dve", lambda e: e.scalar_tensor_tensor(mb, mg[:, 2, 0:128], sm[:, 14:15], acc, ALU.mult, ALU.add),
                         reads=[R_mg[mi], R_sm[mi], R_acc[mi]], writes=[R_mb[mi]])

                def m_b(j, n):
                    mi = mslot[(j, n)]
                    mb = MB[mi]
                    nn = n % 8
                    S.op("pe", lambda e: e.transpose(psb[4][:, nn * 128:(nn + 1) * 128], mb, ident),
                         reads=[R_mb[mi], R_c], writes=[R_ps[4]])
                    if nn == 7:
                        ti = c["mt"] % 2
                        c["mt"] += 1
                        eng = ev_engine()
                        S.op(eng, copy_on(eng, MGTS[ti], psb[4]), reads=[R_ps[4]], writes=[R_mgts[ti]])
                        hb = n // 8
                        S.dma("sp", mgT_d[seq, :, j, hb * TB:(hb + 1) * TB], MGTS[ti], reads=[R_mgts[ti]],
                              writes=[R_mgTd[seq]])

                def merge_steps(j):
                    par = j % 2

                    def step(k):
                        if k == 0:
                            m_load(j, par, 0)
                            m_load(j, par, 1)
                        if k + 2 < 16:
                            m_load(j, par, k + 2)
                        if k < 16:
                            m_a(j, k)
                        if 0 <= k - 2 < 16:
                            m_b(j, k - 2)
                    return [lambda k=k: step(k) for k in range(18)]

                def set_bias(j, g):
                    a = SLOPES[g][j] * DILS[g]
                    S.op("dve", lambda e: e.scalar_tensor_tensor(BIAS[g], negD, a, Mk, ALU.mult, ALU.add),
                         reads=[R_c, R_bias[g]], writes=[R_bias[g]])

                def interleave(tiles, extras):
                    nt_, ne = len(tiles), len(extras)
                    done = 0
                    for ti, tl in enumerate(tiles):
                        tl()
                        want = (ne * (ti + 1)) // nt_ if nt_ else ne
                        while done < want:
                            extras[done]()
                            done += 1
                    while done < ne:
                        extras[done]()
                        done += 1

                def unit_steps(j, g):
                    return [lambda u=u: pipe_step((j, g, u)) for u in range(16)]

                for j in range(8):
                    t1 = proj_tiles(j, 1, 0) + proj_tiles(j, 1, 1)
                    if j > 0:
                        set_bias(j - 1, 2)
                        interleave(t1, unit_steps(j - 1, 2) + [pipe_flush])
                    else:
                        interleave(t1, [])
                    t2 = proj_tiles(j, 1, 2) + proj_tiles(j, 2, 0) + proj_tiles(j, 2, 1) + proj_tiles(j, 2, 2) + \
                        proj_tiles(j, 0, 0)
                    ex = merge_steps(j - 1) if j > 0 else []
                    interleave(t2, ex)
                    set_bias(j, 0)
                    interleave(proj_tiles(j, 0, 1), unit_steps(j, 0))
                    set_bias(j, 1)
                    interleave(proj_tiles(j, 0, 2), unit_steps(j, 1))
                set_bias(7, 2)
                interleave([], unit_steps(7, 2) + [pipe_flush])
                for st_ in merge_steps(7):
                    st_()
                fence_l = [R_x2t] + R_q + R_k + R_vt + [r for a in R_vtok for r in a] + R_bias + R_sb + R_pex + R_pts + \
                    R_ost + R_mg + R_sm + R_acc + R_mb + R_mgts
                fence(fence_l + R_all_ac)

            for seq in range(2):
                phase_A(2 * seq)
                if stop == "A1":
                    break
                phase_A(2 * seq + 1)
                if stop == "A":
                    break
                phase_B(seq)
                if stop == "B":
                    break
                phase_C(2 * seq)
                phase_C(2 * seq + 1)
                if stop == "C":
                    break
            S.barrier("sp", out_toks)

        S.dry = True
        record()
        S.reset()
        del out_toks[:]
        record()
        final = []
        for q in S.dsems:
            n = S.dcnt[q]
            for i, s in enumerate(S.dsems[q]):
                if n > i:
                    final.append((s, 16 * ((n - 1 - i) // KDS + 1)))
        S.barrier("sp", final)
        LAST_COUNTS.update(S.cnt)
        LAST_COUNTS.update({'d_' + q: n for q, n in S.dcnt.items()})
        with nc.Block() as block:
            S.emit(block)
    return nc


_NAMES = ["pool_w_in", "pool_w_group", "pool_scale", "pool_w_out", "attn_w_qkv", "attn_w_out",
          "mlp_w_up", "mlp_w_down", "ln_mix_g", "ln_mix_b", "ln_mlp_g", "ln_mlp_b"]


def kernel(**inputs):
    x = np.ascontiguousarray(np.asarray(inputs["x"], dtype=np.float32))
    n = 8
    xs = x.reshape(n, 4096, D)
    shared = {k: np.ascontiguousarray(np.asarray(inputs[k], dtype=np.float32)) for k in _NAMES}
    nc = build_program()
    in_maps = []
    for c in range(n):
        m = {"x": xs[c]}
        m.update(shared)
        in_maps.append(m)
    res = run_bass_kernel_spmd(nc, in_maps, core_ids=list(range(n)))
    outs = [np.asarray(r["out"], dtype=np.float32).reshape(2, SEQ, D) for r in res.results]
    return np.concatenate(outs, axis=0)
```

```python
import numpy as np
from contextlib import ExitStack
import concourse.bass as bass
import concourse.mybir as mybir
from concourse.bass_utils import run_bass_kernel_spmd

F32 = mybir.dt.float32
BF16 = mybir.dt.bfloat16
AF = mybir.ActivationFunctionType
ALU = mybir.AluOpType
AX = mybir.AxisListType
KDS = 8
SELF_WAIT = True

D = 2048
KD = 16
TB = 1024
NS = 8
SEQ = 2048
DFF = 8192
NB = 4
ALPHA = float(4 ** 0.25)
LN_EPS = 1e-5
ARENA_BYTES = 207 * 1024
DILS = (1, 4, 16)
NBLK = (16, 4, 1)
SLOPES = [[float(2.0 ** (-8.0 * (8 * g + j + 1) / 24.0)) for j in range(8)] for g in range(3)]
QSCALE = float(1.0 / np.sqrt(128.0))
NEG = -30000.0


LAST_COUNTS = {}


class Res:
    __slots__ = ("w", "r")

    def __init__(self):
        self.w = None
        self.r = {}


class Sched:
    def __init__(self, nc, st):
        self.nc = nc
        self.names = ("pe", "act", "dve", "pool", "sp")
        self.sem = {e: st.enter_context(nc.semaphore("s_" + e)) for e in self.names}
        self.dsems = {q: [st.enter_context(nc.semaphore("d_%s%d" % (q, i))) for i in range(KDS)]
                      for q in ("sp", "act", "pool")}
        self.reset()

    def reset(self):
        self.cnt = {e: 0 for e in self.names}
        self.prog = {e: [] for e in self.names}
        self.waited = {e: {} for e in self.names}
        self.dcnt = {q: 0 for q in self.dsems}
        self.dry = False

    def _collect(self, reads, writes):
        toks = []
        for b in reads:
            if b.w is not None:
                toks.append(b.w)
        for b in writes:
            if b.w is not None:
                toks.append(b.w)
            toks.extend(b.r.values())
        return toks

    def _waits(self, e, toks):
        need = {}
        own = self.sem[e]
        for (s, v) in toks:
            if s is own and (e == "pe" or not SELF_WAIT):
                continue
            key = id(s)
            if self.waited[e].get(key, 0) >= v:
                continue
            if need.get(key, (None, 0))[1] < v:
                need[key] = (s, v)
        for key, (s, v) in need.items():
            self.waited[e][key] = v
        return list(need.values())

    def _update(self, tok, reads, writes):
        for b in writes:
            b.w = tok
            b.r = {}
        for b in reads:
            if b not in writes:
                b.r[id(tok[0])] = tok

    def op(self, e, fn, reads=(), writes=()):
        if self.dry:
            return None
        waits = self._waits(e, self._collect(reads, writes))
        self.cnt[e] += 1
        tok = (self.sem[e], self.cnt[e])
        self.prog[e].append((waits, fn, True))
        self._update(tok, reads, writes)
        return tok

    def dma(self, q, out, in_, reads=(), writes=(), **kw):
        if self.dry:
            return None
        i = self.dcnt[q]
        self.dcnt[q] += 1
        s = self.dsems[q][i % KDS]
        v = 16 * (i // KDS + 1)
        toks = self._collect(reads, writes)
        if v > 16:
            toks.append((s, v - 16))
        waits = self._waits(q, toks)
        self.prog[q].append((waits, lambda eng: eng.dma_start(out=out, in_=in_, **kw).then_inc(s, 16), False))
        tok = (s, v)
        self._update(tok, reads, writes)
        return tok

    def barrier(self, e, toks):
        if self.dry:
            return
        self.prog[e].append((self._waits(e, [t for t in toks if t is not None]), None, False))

    def emit(self, block):
        decs = {"pe": block.tensor, "act": block.scalar, "dve": block.vector, "pool": block.gpsimd,
                "sp": block.sync}
        for e in self.names:
            prog = self.prog[e]
            sem = self.sem[e]

            def body(eng, prog=prog, sem=sem):
                for waits, fn, inc in prog:
                    for (s, v) in waits:
                        eng.wait_ge(s, v)
                    if fn is None:
                        continue
                    ins = fn(eng)
                    if inc:
                        ins.then_inc(sem, 1)
            decs[e](body)


class Arena:
    def __init__(self, t, nbytes):
        self.t = t
        self.off = 0
        self.nbytes = nbytes

    def alloc(self, dtype, *free):
        n = int(np.prod(free))
        esz = 2 if dtype is BF16 else 4
        nb = (n * esz + 63) // 64 * 64
        a = self.off
        self.off += nb
        assert self.off <= self.nbytes, ("arena overflow", self.off, self.nbytes)
        v = self.t[:, a // 4:(a + nb) // 4]
        if dtype is BF16:
            v = v.bitcast(BF16)
        v = v[:, 0:n]
        if len(free) == 2:
            v = v.rearrange("p (a b) -> p a b", a=free[0])
        elif len(free) == 3:
            v = v.rearrange("p (a b c) -> p a b c", a=free[0], b=free[1])
        return v


def build_program(stop=None, dbg=False):
    nc = bass.Bass("TRN2", target_bir_lowering=False)

    def din(name, shape):
        return nc.dram_tensor(name, list(shape), F32, kind="ExternalInput").ap()

    x = din("x", (4096, D))
    pool_w_in = din("pool_w_in", (1, D, D))
    pool_w_group = din("pool_w_group", (1, 4, 512, 512))
    pool_scale = din("pool_scale", (1, D))
    pool_w_out = din("pool_w_out", (1, D, D))
    attn_w_qkv = din("attn_w_qkv", (1, D, 9216))
    attn_w_out = din("attn_w_out", (1, 1024, D))
    mlp_w_up = din("mlp_w_up", (2, D, DFF))
    mlp_w_down = din("mlp_w_down", (2, DFF, D))
    ln_mix_g = din("ln_mix_g", (2, D))
    ln_mix_b = din("ln_mix_b", (2, D))
    ln_mlp_g = din("ln_mlp_g", (2, D))
    ln_mlp_b = din("ln_mlp_b", (2, D))
    out = nc.dram_tensor("out", [4096, D], F32, kind="ExternalOutput").ap()
    kind = dict(kind="ExternalOutput") if dbg else {}
    x2_d = nc.dram_tensor("x2_d", [4096, D], F32, **kind).ap()
    x2T_d = nc.dram_tensor("x2T_d", [2, 128, KD, SEQ], BF16).ap()
    og_d = nc.dram_tensor("og_d", [2, 3, SEQ, 132], F32).ap()
    mgT_d = nc.dram_tensor("mgT_d", [2, 128, 8, SEQ], BF16, **kind).ap()

    with ExitStack() as st:
        arena_t = st.enter_context(nc.sbuf_tensor("arena", [128, ARENA_BYTES // 4], F32))
        ps = st.enter_context(nc.psum_tensor("ps", [128, 8, 512], F32))
        S = Sched(nc, st)
        psb = [ps[:, b, :].bitcast(BF16) for b in range(8)]
        specs = []
        out_toks = []

        def record():
            A = Arena(arena_t, ARENA_BYTES)
            WR = [A.alloc(BF16, 8192) for _ in range(NB)]
            R_wr = [[Res(), Res(), Res()] for _ in range(NB)]
            R_ps = [Res() for _ in range(8)]
            wstate = {"i": 0, "issued": 0}

            def wissue(upto):
                while wstate["issued"] < min(upto, len(specs)):
                    p = wstate["issued"]
                    src, shape = specs[p]
                    n = int(np.prod(shape))
                    dst = WR[p % NB][:, 0:n]
                    if len(shape) == 2:
                        dst = dst.rearrange("p (a b) -> p a b", a=shape[0])
                    else:
                        dst = dst.rearrange("p (a b c) -> p a b c", a=shape[0], b=shape[1])
                    if isinstance(src, list):
                        for gi, sp in enumerate(src):
                            S.dma("pool", dst[:, :, gi, :], sp, writes=[R_wr[p % NB][gi]])
                    else:
                        S.dma("pool", dst, src, writes=R_wr[p % NB])
                    wstate["issued"] += 1

            def wnext(src, shape, lag=0):
                i = wstate["i"]
                wstate["i"] += 1
                if S.dry:
                    specs.append((src, shape))
                else:
                    wissue(i + NB - lag)
                n = int(np.prod(shape))
                v = WR[i % NB][:, 0:n]
                if len(shape) == 2:
                    v = v.rearrange("p (a b) -> p a b", a=shape[0])
                else:
                    v = v.rearrange("p (a b c) -> p a b c", a=shape[0], b=shape[1])
                return v, R_wr[i % NB]

            ident = A.alloc(BF16, 128)
            identf = A.alloc(F32, 128)
            negD = A.alloc(F32, 256)
            Mk = A.alloc(F32, 256)
            ctmp = A.alloc(F32, 256)
            invc = A.alloc(F32, 4, 16)
            epst = A.alloc(F32, 1)
            scol = A.alloc(F32, 16)
            hist = A.alloc(F32, 16, 16)
            FZ = A.alloc(F32, 1)
            R_c = Res()
            R_hist = [Res() for _ in range(16)]
            R_x2d = [Res() for _ in range(4)]
            R_x2Td = [Res(), Res()]
            R_mgTd = [Res(), Res()]

            def fence(res):
                tok = S.op("pool", lambda e: e.memset(FZ, 0.0), writes=res)
                for en in S.names:
                    S.barrier(en, [tok])
            S.op("pool", lambda e: e.iota(identf, [[1, 128]], base=0, channel_multiplier=-1,
                                          allow_small_or_imprecise_dtypes=True), writes=[R_c])
            S.op("dve", lambda e: e.tensor_scalar(ident, identf, 0.0, None, ALU.is_equal), reads=[R_c], writes=[R_c])
            S.op("pool", lambda e: e.iota(negD, [[1, 256]], base=-128, channel_multiplier=-1,
                                          allow_small_or_imprecise_dtypes=True), writes=[R_c])
            S.op("dve", lambda e: e.tensor_scalar(Mk, negD, 0.0, None, ALU.is_le), reads=[R_c], writes=[R_c])
            S.op("dve", lambda e: e.tensor_scalar(ctmp, negD, -128.0, None, ALU.is_ge), reads=[R_c], writes=[R_c])
            S.op("dve", lambda e: e.tensor_tensor(Mk, Mk, ctmp, ALU.mult), reads=[R_c], writes=[R_c])
            S.op("dve", lambda e: e.tensor_scalar(Mk, Mk, -1.0, -NEG, ALU.add, ALU.mult), reads=[R_c], writes=[R_c])
            S.op("pool", lambda e: e.iota(ctmp[:, 0:16], [[1, 16]], base=1, channel_multiplier=0,
                                          allow_small_or_imprecise_dtypes=True), reads=[R_c], writes=[R_c])
            for g in range(4):
                S.op("dve", lambda e, g=g: e.tensor_scalar(invc[:, g, :], ctmp[:, 0:16], float(2 ** (g + 1)), None, ALU.min),
                     reads=[R_c], writes=[R_c])
                S.op("dve", lambda e, g=g: e.reciprocal(invc[:, g, :], invc[:, g, :]), reads=[R_c], writes=[R_c])
            S.op("dve", lambda e: e.memset(epst, LN_EPS), writes=[R_c])
            S.dma("sp", scol, pool_scale[0].rearrange("(j p) -> p j", p=128), writes=[R_c],
                  allow_slow_non_contiguous=True)

            region0 = A.off

            XRES = A.alloc(F32, NS, D)
            XT = A.alloc(BF16, KD, TB)
            HT = [A.alloc(BF16, 4, TB) for _ in range(2)]
            GT = A.alloc(F32, D)
            BT = A.alloc(F32, D)
            XB = [A.alloc(BF16, D) for _ in range(2)]
            STT = A.alloc(F32, 8, NS)
            BS = [A.alloc(F32, 4, 6) for _ in range(4)]
            STS = [A.alloc(F32, 8) for _ in range(4)]
            R_bs = [Res() for _ in range(4)]
            regionAC_end = A.off
            A.off = region0 + NS * D * 4 + KD * TB * 2 + 2 * 4 * TB * 2
            UT = [A.alloc(F32, 528) for _ in range(2)]
            SA = [A.alloc(F32, 528) for _ in range(2)]
            SB = [A.alloc(F32, 528) for _ in range(2)]
            T16 = A.alloc(F32, 16)
            assert A.off <= regionAC_end
            A.off = region0 + NS * D * 4 + KD * TB * 2 + 2 * 4 * TB * 2 + D * 4
            RT = [A.alloc(F32, 512) for _ in range(2)]
            A.off = regionAC_end
            R_u = [Res(), Res()]
            R_xres = [[Res() for _ in range(4)] for _ in range(NS)]
            R_xt = [Res() for _ in range(NS)]
            R_ht = [[[Res() for _ in range(2)] for _ in range(4)] for _ in range(2)]
            R_gb = Res()
            R_bt = Res()
            R_xb = [Res(), Res()]
            R_rt = [Res(), Res()]
            R_st = Res()
            cnt = {"a": 0, "t": 0, "xb": 0, "rt": 0, "u": 0, "ev": 0, "bs": 0, "xb4": 0}

            def ev_engine():
                cnt["ev"] += 1
                return "act" if cnt["ev"] % 2 else "dve"

            def copy_on(e, o, i):
                if e == "act":
                    return lambda eng: eng.activation(o, i, AF.Copy)
                return lambda eng: eng.tensor_copy(o, i)

            def transposes(src_bf, R_src, s, evac=None):
                for half in range(2):
                    bank = 2 + cnt["t"] % 2
                    cnt["t"] += 1
                    for kk in range(8):
                        k = half * 8 + kk
                        S.op("pe", lambda e, k=k, kk=kk, bank=bank: e.transpose(
                            psb[bank][:, kk * 128:(kk + 1) * 128], src_bf[:, k * 128:(k + 1) * 128], ident),
                            reads=(R_src if isinstance(R_src, list) else [R_src]) + [R_c], writes=[R_ps[bank]])
                    eng = evac or ev_engine()
                    S.op(eng, copy_on(eng, XT[:, half * 8:(half + 1) * 8, s * 128:(s + 1) * 128],
                                      psb[bank].rearrange("p (a b) -> p a b", a=8)),
                         reads=[R_ps[bank]], writes=[R_xt[s]])

            def cast_and_transpose(s):
                i = cnt["xb"] % 2
                cnt["xb"] += 1
                S.op("act", lambda e, i=i, s=s: e.activation(XB[i], XRES[:, s, :], AF.Copy),
                     reads=R_xres[s], writes=[R_xb[i]])
                transposes(XB[i], R_xb[i], s)

            class LN:
                LAGS = (("A", 0), ("B", 1), ("C", 1), ("D", 2), ("E", 2), ("F", 3), ("G", 4))

                def __init__(self, g_ap, b_ap, post, need_xt=True):
                    self.g_ap, self.b_ap, self.post, self.need_xt = g_ap, b_ap, post, need_xt

                def pre(self):
                    S.dma("sp", GT, self.g_ap.partition_broadcast(128), writes=[R_gb])
                    S.dma("sp", BT, self.b_ap.partition_broadcast(128), writes=[R_bt])

                def A(self, s):
                    bs, st_, R_b = BS[s % 4], STS[s % 4], R_bs[s % 4]
                    for c4 in range(4):
                        S.op("dve", lambda e, c4=c4: e.bn_stats(bs[:, c4, :], XRES[:, s, c4 * 512:(c4 + 1) * 512]),
                             reads=[R_xres[s][c4]], writes=[R_b])
                    S.op("dve", lambda e: e.bn_aggr(st_[:, 0:2], bs.rearrange("p a b -> p (a b)")), reads=[R_b], writes=[R_b])

                def B(self, s):
                    st_, R_b = STS[s % 4], R_bs[s % 4]
                    S.op("act", lambda e: e.activation(st_[:, 2:3], st_[:, 1:2], AF.Sqrt, bias=epst[:, 0:1]),
                         reads=[R_b, R_c], writes=[R_b])
                    S.op("dve", lambda e: e.reciprocal(st_[:, 3:4], st_[:, 2:3]), reads=[R_b], writes=[R_b])
                    S.op("dve", lambda e: e.scalar_tensor_tensor(st_[:, 4:5], st_[:, 0:1], -1.0, st_[:, 3:4],
                                                                 ALU.mult, ALU.mult), reads=[R_b], writes=[R_b])

                def C(self, s):
                    st_, R_b = STS[s % 4], R_bs[s % 4]
                    S.op("act", lambda e: e.activation(XRES[:, s, :], XRES[:, s, :], AF.Identity,
                                                       scale=st_[:, 3:4], bias=st_[:, 4:5]),
                         reads=[R_b] + R_xres[s], writes=R_xres[s])

                def D(self, s):
                    S.op("dve", lambda e: e.tensor_tensor(XRES[:, s, :], XRES[:, s, :], GT, ALU.mult),
                         reads=[R_gb] + R_xres[s], writes=R_xres[s])

                def E(self, s):
                    S.op("pool", lambda e: e.tensor_tensor(XRES[:, s, :], XRES[:, s, :], BT, ALU.add),
                         reads=[R_bt] + R_xres[s], writes=R_xres[s])

                def F(self, s):
                    if self.need_xt:
                        S.op("act", lambda e: e.activation(XB[s % 2], XRES[:, s, :], AF.Copy),
                             reads=R_xres[s], writes=[R_xb[s % 2]])

                def G(self, s):
                    if self.need_xt:
                        transposes(XB[s % 2], R_xb[s % 2], s, evac="act")
                    self.post(s)

                def step(self, t):
                    for name, lag in self.LAGS:
                        sp = t - lag
                        if 0 <= sp < NS:
                            getattr(self, name)(sp)

                def after(self, s):
                    self.step(s)
                    if s == NS - 1:
                        for t in range(NS, NS + 4):
                            self.step(t)

            def acc_evac(s, nb, first):
                o = XRES[:, s, nb * 512:(nb + 1) * 512]
                if first:
                    S.op("dve", lambda e: e.scalar_tensor_tensor(o, o, ALPHA, ps[:, 4 + nb, :], ALU.mult, ALU.add),
                         reads=[R_ps[4 + nb], R_xres[s][nb]], writes=[R_xres[s][nb]])
                else:
                    S.op("dve", lambda e: e.tensor_tensor(o, o, ps[:, 4 + nb, :], ALU.add),
                         reads=[R_ps[4 + nb], R_xres[s][nb]], writes=[R_xres[s][nb]])

            def down_proj(hbuf, R_h, wd, R_wd, nk, first, wsel=None, ln=None):
                for s in range(NS):
                    for nb in range(4):
                        for kk in range(nk):
                            w_ap, R_w = (wd[:, kk, nb * 512:(nb + 1) * 512], R_wd) if wsel is None else wsel(kk, nb)
                            S.op("pe", lambda e, s=s, nb=nb, kk=kk, w_ap=w_ap: e.matmul(
                                ps[:, 4 + nb, :], hbuf[:, kk, s * 128:(s + 1) * 128], w_ap,
                                start=(kk == 0), stop=(kk == nk - 1)),
                                reads=[R_h(kk, s)] + R_w, writes=[R_ps[4 + nb]])
                        acc_evac(s, nb, first)
                    if ln is not None:
                        ln.after(s)

            def up_tile(wv, R_w, jj, half, nk, rhs_of, R_rhs):
                bank = cnt["a"] % 2
                cnt["a"] += 1
                for k in range(nk):
                    S.op("pe", lambda e, k=k, bank=bank: e.matmul(
                        ps[:, bank, :], wv[:, k, jj * 128:(jj + 1) * 128], rhs_of(k, half),
                        start=(k == 0), stop=(k == nk - 1)),
                        reads=R_w + R_rhs(half), writes=[R_ps[bank]])
                return bank

            def xt_rhs(k, half):
                return XT[:, k, half * 512:(half + 1) * 512]

            def R_xt_half(half):
                return R_xt[4 * half:4 * half + 4]

            def mlp(layer, ln, prefetch=None):
                wu_all = mlp_w_up[layer].rearrange("(k p) c -> p k c", p=128)
                wd_all = mlp_w_down[layer].rearrange("(k p) c -> p k c", p=128)
                for f in range(DFF // 512):
                    wu, R_wu = wnext(wu_all[:, :, f * 512:(f + 1) * 512], (KD, 512))
                    hb = f % 2
                    for half in range(2):
                        for jj in range(4):
                            bank = up_tile(wu, R_wu, jj, half, KD, xt_rhs, R_xt_half)
                            ri = cnt["rt"] % 2
                            cnt["rt"] += 1
                            S.op("act", lambda e, bank=bank, ri=ri: e.activation(RT[ri], ps[:, bank, :], AF.Relu),
                                 reads=[R_ps[bank]], writes=[R_rt[ri], R_bt])
                            S.op("dve", lambda e, ri=ri, jj=jj, half=half, hb=hb: e.tensor_tensor(
                                HT[hb][:, jj, half * 512:(half + 1) * 512], RT[ri], RT[ri], ALU.mult),
                                reads=[R_rt[ri], R_bt], writes=[R_ht[hb][jj][half]])
                    last = (f == DFF // 512 - 1)
                    if last and prefetch is not None:
                        prefetch()
                    wd, R_wd = wnext(wd_all[:, 4 * f:4 * f + 4, :], (4, D))
                    if last:
                        ln.pre()
                    down_proj(HT[hb], lambda kk, s, hb=hb: R_ht[hb][kk][s // 4], wd, R_wd, 4, first=(f == 0),
                              ln=ln if last else None)

            def pool_mixer(seq_start, ln):
                win_all = pool_w_in[0].rearrange("(k p) c -> p k c", p=128)
                wout_all = pool_w_out[0].rearrange("(k p) c -> p k c", p=128)
                for g in range(4):
                    w = 2 ** (g + 1)
                    win, R_win = wnext(win_all[:, :, g * 512:(g + 1) * 512], (KD, 512))
                    PT, R_pt = HT[0], R_ht[0]
                    Y2, R_y2 = HT[1], R_ht[1]
                    for half in range(2):
                        for jj in range(4):
                            j = 4 * g + jj
                            bank = up_tile(win, R_win, jj, half, KD, xt_rhs, R_xt_half)
                            ui = cnt["u"] % 2
                            cnt["u"] += 1
                            U, Sa, Sb = UT[ui], SA[ui], SB[ui]
                            Ru = R_u[ui]
                            S.op("act", lambda e, U=U, bank=bank: e.activation(U[:, 16:528], ps[:, bank, :], AF.Copy),
                                 reads=[R_ps[bank]], writes=[Ru, R_gb, R_bt])
                            if seq_start and half == 0:
                                S.op("pool", lambda e, U=U: e.memset(U[:, 0:16], 0.0), reads=[R_hist[j]], writes=[Ru])
                            else:
                                S.op("pool", lambda e, U=U, j=j: e.tensor_copy(U[:, 0:16], hist[:, j, :]),
                                     reads=[R_hist[j]], writes=[Ru])
                            S.op("pool", lambda e, U=U, j=j: e.tensor_copy(hist[:, j, :], U[:, 512:528]),
                                 reads=[Ru], writes=[R_hist[j]])
                            S.op("dve", lambda e, U=U, Sa=Sa: e.tensor_tensor(Sa[:, 1:528], U[:, 1:528], U[:, 0:527], ALU.add),
                                 reads=[Ru], writes=[Ru])
                            fin = Sa
                            if g >= 1:
                                S.op("dve", lambda e, Sa=Sa, Sb=Sb: e.tensor_tensor(Sb[:, 3:528], Sa[:, 3:528], Sa[:, 1:526], ALU.add),
                                     reads=[Ru], writes=[Ru])
                                fin = Sb
                            if g >= 2:
                                S.op("dve", lambda e, Sa=Sa, Sb=Sb: e.tensor_tensor(Sa[:, 7:528], Sb[:, 7:528], Sb[:, 3:524], ALU.add),
                                     reads=[Ru], writes=[Ru])
                                fin = Sa
                            if g >= 3:
                                S.op("dve", lambda e, Sa=Sa, Sb=Sb: e.tensor_tensor(Sb[:, 15:528], Sa[:, 15:528], Sa[:, 7:520], ALU.add),
                                     reads=[Ru], writes=[Ru])
                                fin = Sb
                            pdst = PT[:, jj, half * 512:(half + 1) * 512]
                            S.op("dve", lambda e, fin=fin, U=U, pdst=pdst, w=w: e.scalar_tensor_tensor(
                                pdst, fin[:, 16:528], 1.0 / w, U[:, 16:528], ALU.mult, ALU.subtract),
                                reads=[Ru, R_gb, R_bt], writes=[R_pt[jj][half]])
                            if seq_start and half == 0:
                                S.op("dve", lambda e, fin=fin, g=g: e.tensor_tensor(T16, fin[:, 16:32], invc[:, g, :], ALU.mult),
                                     reads=[Ru, R_c], writes=[R_st])
                                S.op("dve", lambda e, U=U, jj=jj: e.tensor_tensor(PT[:, jj, 0:16], T16, U[:, 16:32], ALU.subtract),
                                     reads=[Ru, R_st, R_gb, R_bt], writes=[R_pt[jj][half]])
                    wg, R_wg = wnext(pool_w_group[0, g].rearrange("(k p) c -> p k c", p=128), (4, 512))
                    for half in range(2):
                        for jj in range(4):
                            j = 4 * g + jj
                            bank = up_tile(wg, R_wg, jj, half, 4, lambda k, half: PT[:, k, half * 512:(half + 1) * 512],
                                           lambda half: [R_pt[k][half] for k in range(4)])
                            S.op("act", lambda e, bank=bank, jj=jj, half=half, j=j: e.activation(
                                Y2[:, jj, half * 512:(half + 1) * 512], ps[:, bank, :], AF.Identity, scale=scol[:, j:j + 1]),
                                reads=[R_ps[bank], R_c], writes=[R_y2[jj][half]])
                    wo, R_wo = wnext(wout_all[:, 4 * g:4 * g + 4, :], (4, D))
                    if g == 3:
                        ln.pre()
                    down_proj(Y2, lambda kk, s: R_y2[kk][s // 4], wo, R_wo, 4, first=(g == 0), ln=ln if g == 3 else None)

            def load_mgT(seq, hb):
                S.dma("sp", XT[:, 0:8, :], mgT_d[seq, :, :, hb * TB:(hb + 1) * TB], reads=[R_mgTd[seq]], writes=R_xt)

            def attn_out(seq, hb, ln, preloaded=False):
                if not preloaded:
                    load_mgT(seq, hb)
                wo_all = attn_w_out[0].rearrange("(k p) c -> p k c", p=128)
                w0, R_w0 = wnext(wo_all[:, :, 0:1024], (8, 1024))
                w1, R_w1 = wnext(wo_all[:, :, 1024:2048], (8, 1024), lag=1)

                def wsel(kk, nb):
                    wv, R_w = (w0, R_w0) if nb < 2 else (w1, R_w1)
                    return wv[:, kk, (nb % 2) * 512:(nb % 2 + 1) * 512], R_w
                ln.pre()
                down_proj(XT, lambda kk, s: R_xt[s], None, None, 8, first=True, wsel=wsel, ln=ln)

            def load_block(src, tok0, rd=()):
                for s in range(NS):
                    S.dma("sp", XRES[:, s, :], src[tok0 + s * 128:tok0 + (s + 1) * 128, :],
                          reads=list(rd), writes=R_xres[s])

            def phase_A(blk):
                tok0 = blk * TB
                seq, hb = blk // 2, blk % 2
                load_block(x, tok0)
                for s in range(NS):
                    cast_and_transpose(s)
                pool_mixer((hb == 0), LN(ln_mix_g[0], ln_mix_b[0], lambda s: None))

                def post(s):
                    S.dma("sp", x2_d[tok0 + s * 128: tok0 + (s + 1) * 128, :], XRES[:, s, :], reads=R_xres[s], writes=[R_x2d[blk]])
                    if s == NS - 1:
                        S.dma("sp", x2T_d[seq, :, :, hb * TB:(hb + 1) * TB], XT, reads=R_xt, writes=[R_x2Td[seq]])
                mlp(0, LN(ln_mlp_g[0], ln_mlp_b[0], post))

            def phase_C(blk):
                tok0 = blk * TB
                seq, hb = blk // 2, blk % 2
                load_block(x2_d, tok0, [R_x2d[blk]])
                attn_out(seq, hb, LN(ln_mix_g[1], ln_mix_b[1], lambda s: None), preloaded=(hb == 1))

                def post(s):
                    out_toks.append(S.dma("sp", out[tok0 + s * 128: tok0 + (s + 1) * 128, :], XRES[:, s, :],
                                          reads=R_xres[s]))
                mlp(1, LN(ln_mlp_g[1], ln_mlp_b[1], post, need_xt=False),
                    prefetch=(lambda: load_mgT(seq, 1)) if hb == 0 else None)

            def phase_B(seq):
                A.off = region0
                X2T = A.alloc(BF16, KD, SEQ)
                QT = [A.alloc(BF16, SEQ) for _ in range(3)]
                KT = [A.alloc(BF16, SEQ) for _ in range(3)]
                VT = [A.alloc(BF16, SEQ) for _ in range(2)]
                VTOK = A.alloc(BF16, 3, 16, 128)
                BIAS = [A.alloc(F32, 256) for _ in range(3)]
                NSB, NR, NO, NM = 3, 5, 9, 4
                SK2, SK3 = 3, 6
                SBt = [A.alloc(F32, 256) for _ in range(NSB)]
                PEX = [A.alloc(BF16, 256) for _ in range(NR)]
                PTS = [A.alloc(BF16, 2, 128) for _ in range(NR)]
                OST = [A.alloc(F32, 132) for _ in range(NO)]
                MG = [A.alloc(F32, 3, 132) for _ in range(NM)]
                SM = [A.alloc(F32, 16) for _ in range(NM)]
                ACC = [A.alloc(F32, 128) for _ in range(NM)]
                MB = [A.alloc(BF16, 128) for _ in range(NM)]
                MGTS = [A.alloc(BF16, TB) for _ in range(2)]
                R_all_ac = [r for rs in R_xres for r in rs] + R_xt + [r for a in R_ht for b in a for r in b] + \
                    [R_gb, R_bt, R_st] + R_xb + R_rt + R_bs + R_u
                R_x2t4 = [Res() for _ in range(4)]
                R_q = [Res() for _ in range(3)]
                R_k = [Res() for _ in range(3)]
                R_vt = [Res(), Res()]
                R_vtok = [[Res() for _ in range(16)] for _ in range(3)]
                R_bias = [Res() for _ in range(3)]
                R_sb = [Res() for _ in range(NSB)]
                R_pex = [Res() for _ in range(NR)]
                R_pts = [Res() for _ in range(NR)]
                R_ost = [Res() for _ in range(NO)]
                R_mg = [Res() for _ in range(NM)]
                R_sm = [Res() for _ in range(NM)]
                R_acc = [Res() for _ in range(NM)]
                R_mb = [Res() for _ in range(NM)]
                R_mgts = [Res(), Res()]
                R_og = [[[Res() for _ in range(16)] for _ in range(3)] for _ in range(2)]
                fence(R_all_ac)
                for tq in range(4):
                    S.dma("sp", X2T[:, :, tq * 512:(tq + 1) * 512], x2T_d[seq, :, :, tq * 512:(tq + 1) * 512],
                          reads=[R_x2Td[seq]], writes=[R_x2t4[tq]])
                wq_all = attn_w_qkv[0].rearrange("(k p) (c g h e) -> p k c g h e", p=128, c=3, g=3, h=8)
                psS = [ps[:, 6, 0:256], ps[:, 7, 0:256], ps[:, 6, 256:512], ps[:, 7, 256:512]]
                c = {"vt": 0, "u": 0, "pt": 0, "po": 0, "mg": 0, "mt": 0}
                wcur = {}

                def proj_tiles(j, cc, g):
                    def first():
                        if cc not in wcur or wcur[cc][0] != j:
                            wv, R_w = wnext([wq_all[:, :, cc, gi, j, :] for gi in range(3)], (KD, 3, 128))
                            wcur[cc] = (j, wv, R_w)
                        if cc == 2:
                            wcur["vi"] = c["vt"] % 2
                            c["vt"] += 1

                    def tile(tq):
                        if tq == 0:
                            first()
                        _, wv, R_w = wcur[cc]
                        d = DILS[g]
                        if cc == 2:
                            vi = wcur["vi"]
                            dstT, R_dst = VT[vi], R_vt[vi]
                        else:
                            dstT, R_dst = (QT[g], R_q[g]) if cc == 0 else (KT[g], R_k[g])
                        bank = cnt["a"] % 2
                        cnt["a"] += 1
                        for k in range(KD):
                            S.op("pe", lambda e, k=k: e.matmul(
                                ps[:, bank, :], wv[:, k, g, :], X2T[:, k, tq * 512:(tq + 1) * 512],
                                start=(k == 0), stop=(k == KD - 1)),
                                reads=R_w + [R_x2t4[tq]], writes=[R_ps[bank]])
                        n = 512 // d
                        dst = dstT.rearrange("p (r i) -> p r i", r=d)[:, :, tq * n:(tq + 1) * n]
                        src = ps[:, bank, :].rearrange("p (i r) -> p r i", r=d)
                        eng = ev_engine()
                        if cc == 0:
                            if eng == "act":
                                S.op("act", lambda e: e.activation(dst, src, AF.Identity, scale=QSCALE),
                                     reads=[R_ps[bank]], writes=[R_dst])
                            else:
                                S.op("dve", lambda e: e.tensor_scalar(dst, src, QSCALE, None, ALU.mult),
                                     reads=[R_ps[bank]], writes=[R_dst])
                        else:
                            S.op(eng, copy_on(eng, dst, src), reads=[R_ps[bank]], writes=[R_dst])
                        if cc == 2 and tq == 3:
                            for u0 in (0, 8):
                                for uu in range(8):
                                    u = u0 + uu
                                    S.op("pe", lambda e, u=u, uu=uu: e.transpose(
                                        psb[5][:, uu * 128:(uu + 1) * 128], dstT[:, u * 128:(u + 1) * 128], ident),
                                        reads=[R_dst, R_c], writes=[R_ps[5]])
                                eng = ev_engine()
                                S.op(eng, copy_on(eng, VTOK[:, g, u0:u0 + 8, :],
                                                  psb[5].rearrange("p (a b) -> p a b", a=8)),
                                     reads=[R_ps[5]], writes=R_vtok[g][u0:u0 + 8])
                    return [lambda tq=tq: tile(tq) for tq in range(4)]

                def unit_s1(j, g, u):
                    b = u % NBLK[g]
                    nk = 256 if b > 0 else 128
                    i = c["u"] % NR
                    bi = c["u"] % NSB
                    si = c["u"] % 4
                    oi = c["u"] % NO
                    c["u"] += 1
                    k0 = (u - 1) * 128 if b > 0 else u * 128
                    bank = 6 + si % 2
                    S.op("pe", lambda e: e.matmul(psS[si][:, 0:nk], QT[g][:, u * 128:(u + 1) * 128],
                                                  KT[g][:, k0:k0 + nk], start=True, stop=True),
                         reads=[R_q[g], R_k[g]], writes=[R_ps[bank]])
                    bsl = BIAS[g][:, 0:256] if b > 0 else BIAS[g][:, 128:256]
                    S.op("dve", lambda e: e.tensor_tensor(SBt[bi][:, 0:nk], psS[si][:, 0:nk], bsl, ALU.add),
                         reads=[R_ps[bank], R_bias[g]], writes=[R_sb[bi]])
                    S.op("dve", lambda e: e.tensor_reduce(OST[oi][:, 129:130], SBt[bi][:, 0:nk], AX.X, ALU.max, negate=True),
                         reads=[R_sb[bi]], writes=[R_ost[oi]])
                    S.op("act", lambda e: e.activation(PEX[i][:, 0:nk], SBt[bi][:, 0:nk], AF.Exp, bias=OST[oi][:, 129:130],
                                                       accum_out=OST[oi][:, 128:129]),
                         reads=[R_sb[bi], R_ost[oi]], writes=[R_pex[i], R_ost[oi]])
                    return dict(j=j, g=g, u=u, b=b, nk=nk, i=i, oi=oi, par=j % 2)

                def unit_s2(t):
                    i, nk = t["i"], t["nk"]
                    nt = nk // 128
                    q4 = c["pt"] % 4
                    c["pt"] += 1
                    pv = psb[2][:, q4 * 256:(q4 + 1) * 256]
                    for cc in range(nt):
                        S.op("pe", lambda e, cc=cc: e.transpose(pv[:, cc * 128:(cc + 1) * 128],
                                                               PEX[i][:, cc * 128:(cc + 1) * 128], ident),
                             reads=[R_pex[i], R_c], writes=[R_ps[2]])
                    S.op("dve", copy_on("dve", PTS[i][:, 0:nt, :], pv[:, 0:nk].rearrange("p (a b) -> p a b", a=nt)),
                         reads=[R_ps[2]], writes=[R_pts[i]])

                def unit_s3(t):
                    i, nk, g, u, b, oi, par = t["i"], t["nk"], t["g"], t["u"], t["b"], t["oi"], t["par"]
                    nt = nk // 128
                    q4 = c["po"] % 4
                    c["po"] += 1
                    po = ps[:, 3, q4 * 128:(q4 + 1) * 128]
                    for cc in range(nt):
                        vu = (u - 1 + cc) if b > 0 else u
                        S.op("pe", lambda e, cc=cc, vu=vu: e.matmul(po, PTS[i][:, cc, :], VTOK[:, g, vu, :],
                                                                  start=(cc == 0), stop=(cc == nt - 1)),
                             reads=[R_pts[i], R_vtok[g][vu]], writes=[R_ps[3]])
                    S.op("act", lambda e: e.activation(OST[oi][:, 0:128], po, AF.Copy),
                         reads=[R_ps[3], R_ost[oi]], writes=[R_ost[oi]])
                    d = DILS[g]
                    r = u // NBLK[g]
                    dst = og_d[par, g].rearrange("(i r) c -> r i c", r=d)[r, 128 * b:128 * b + 128, 0:130]
                    S.dma("sp", dst, OST[oi][:, 0:130], reads=[R_ost[oi]], writes=[R_og[par][g][u]])

                pipe = []
                pst = {"n": 0}

                def pipe_step(unit):
                    t = pst["n"]
                    pst["n"] += 1
                    if t >= SK2 and pipe[t - SK2] is not None:
                        unit_s2(pipe[t - SK2])
                    if t >= SK3 and pipe[t - SK3] is not None:
                        unit_s3(pipe[t - SK3])
                    pipe.append(unit_s1(*unit) if unit is not None else None)

                def pipe_flush():
                    for _ in range(SK3):
                        pipe_step(None)

                mslot = {}

                def m_load(j, par, n):
                    all_og = [R_og[par][g][u] for g in range(3) for u in range(16)]
                    mi = c["mg"] % NM
                    c["mg"] += 1
                    mslot[(j, n)] = mi
                    S.dma("sp", MG[mi], og_d[par, :, n * 128:(n + 1) * 128, :].rearrange("g t c -> t g c"),
                          reads=all_og, writes=[R_mg[mi]])

                def m_a(j, n):
                    mi = mslot[(j, n)]
                    mg, sm = MG[mi], SM[mi]
                    S.op("act", lambda e: e.activation(sm[:, 0:3], mg[:, :, 128], AF.Ln),
                         reads=[R_mg[mi]], writes=[R_sm[mi]])
                    S.op("dve", lambda e: e.tensor_tensor(sm[:, 0:3], sm[:, 0:3], mg[:, :, 129], ALU.subtract),
                         reads=[R_mg[mi], R_sm[mi]], writes=[R_sm[mi]])
                    S.op("dve", lambda e: e.tensor_reduce(sm[:, 3:4], sm[:, 0:3], AX.X, ALU.max, negate=True),
                         reads=[R_sm[mi]], writes=[R_sm[mi]])
                    S.op("act", lambda e: e.activation(sm[:, 4:7], sm[:, 0:3], AF.Exp, bias=sm[:, 3:4],
                                                       accum_out=sm[:, 7:8]),
                         reads=[R_sm[mi]], writes=[R_sm[mi]])
                    S.op("dve", lambda e: e.reciprocal(sm[:, 8:11], mg[:, :, 128]),
                         reads=[R_mg[mi], R_sm[mi]], writes=[R_sm[mi]])
                    S.op("dve", lambda e: e.reciprocal(sm[:, 11:12], sm[:, 7:8]),
                         reads=[R_sm[mi]], writes=[R_sm[mi]])
                    S.op("dve", lambda e: e.tensor_tensor(sm[:, 12:15], sm[:, 4:7], sm[:, 8:11], ALU.mult),
                         reads=[R_sm[mi]], writes=[R_sm[mi]])
                    S.op("dve", lambda e: e.tensor_scalar(sm[:, 12:15], sm[:, 12:15], sm[:, 11:12], None, ALU.mult),
                         reads=[R_sm[mi]], writes=[R_sm[mi]])
                    acc, mb = ACC[mi], MB[mi]
                    S.op("act", lambda e: e.activation(acc, mg[:, 0, 0:128], AF.Identity, scale=sm[:, 12:13]),
                         reads=[R_mg[mi], R_sm[mi]], writes=[R_acc[mi]])
                    S.op("dve", lambda e: e.scalar_tensor_tensor(acc, mg[:, 1, 0:128], sm[:, 13:14], acc, ALU.mult, ALU.add),
                         reads=[R_mg[mi], R_sm[mi], R_acc[mi]], writes=[R_acc[mi]])
                    S.op("dve", lambda e: e.scalar_tensor_tensor(mb, mg[:, 2, 0:128], sm[:, 14:15], acc, ALU.mult, ALU.add),
                         reads=[R_mg[mi], R_sm[mi], R_acc[mi]], writes=[R_mb[mi]])

                def m_b(j, n):
                    mi = mslot[(j, n)]
                    mb = MB[mi]
                    nn = n % 8
                    S.op("pe", lambda e: e.transpose(psb[4][:, nn * 128:(nn + 1) * 128], mb, ident),
                         reads=[R_mb[mi], R_c], writes=[R_ps[4]])
                    if nn == 7:
                        ti = c["mt"] % 2
                        c["mt"] += 1
                        eng = ev_engine()
                        S.op(eng, copy_on(eng, MGTS[ti], psb[4]), reads=[R_ps[4]], writes=[R_mgts[ti]])
                        hb = n // 8
                        S.dma("sp", mgT_d[seq, :, j, hb * TB:(hb + 1) * TB], MGTS[ti], reads=[R_mgts[ti]],
                              writes=[R_mgTd[seq]])

                def merge_steps(j):
                    par = j % 2

                    def step(k):
                        if k == 0:
                            m_load(j, par, 0)
                            m_load(j, par, 1)
                        if k + 2 < 16:
                            m_load(j, par, k + 2)
                        if k < 16:
                            m_a(j, k)
                        if 0 <= k - 2 < 16:
                            m_b(j, k - 2)
                    return [lambda k=k: step(k) for k in range(18)]

                def set_bias(j, g):
                    a = SLOPES[g][j] * DILS[g]
                    S.op("dve", lambda e: e.scalar_tensor_tensor(BIAS[g], negD, a, Mk, ALU.mult, ALU.add),
                         reads=[R_c, R_bias[g]], writes=[R_bias[g]])

                def interleave(tiles, extras):
                    nt_, ne = len(tiles), len(extras)
                    done = 0
                    for ti, tl in enumerate(tiles):
                        tl()
                        want = (ne * (ti + 1)) // nt_ if nt_ else ne
                        while done < want:
                            extras[done]()
                            done += 1
                    while done < ne:
                        extras[done]()
                        done += 1

                def unit_steps(j, g):
                    return [lambda u=u: pipe_step((j, g, u)) for u in range(16)]

                for j in range(8):
                    t1 = proj_tiles(j, 1, 0) + proj_tiles(j, 1, 1)
                    if j > 0:
                        set_bias(j - 1, 2)
                        interleave(t1, unit_steps(j - 1, 2) + [pipe_flush])
                    else:
                        interleave(t1, [])
                    t2 = proj_tiles(j, 1, 2) + proj_tiles(j, 2, 0) + proj_tiles(j, 2, 1) + proj_tiles(j, 2, 2) + \
                        proj_tiles(j, 0, 0)
                    ex = merge_steps(j - 1) if j > 0 else []
                    interleave(t2, ex)
                    set_bias(j, 0)
                    interleave(proj_tiles(j, 0, 1), unit_steps(j, 0))
                    set_bias(j, 1)
                    interleave(proj_tiles(j, 0, 2), unit_steps(j, 1))
                set_bias(7, 2)
                interleave([], unit_steps(7, 2) + [pipe_flush])
                for st_ in merge_steps(7):
                    st_()
                fence_l = R_x2t4 + R_q + R_k + R_vt + [r for a in R_vtok for r in a] + R_bias + R_sb + R_pex + R_pts + \
                    R_ost + R_mg + R_sm + R_acc + R_mb + R_mgts
                fence(fence_l + R_all_ac)

            for seq in range(2):
                phase_A(2 * seq)
                if stop == "A1":
                    break
                phase_A(2 * seq + 1)
                if stop == "A":
                    break
                phase_B(seq)
                if stop == "B":
                    break
                phase_C(2 * seq)
                phase_C(2 * seq + 1)
                if stop == "C":
                    break
            S.barrier("sp", out_toks)

        S.dry = True
        record()
        S.reset()
        del out_toks[:]
        record()
        final = []
        for q in S.dsems:
            n = S.dcnt[q]
            for i, s in enumerate(S.dsems[q]):
                if n > i:
                    final.append((s, 16 * ((n - 1 - i) // KDS + 1)))
        S.barrier("sp", final)
        LAST_COUNTS.update(S.cnt)
        LAST_COUNTS.update({'d_' + q: n for q, n in S.dcnt.items()})
        with nc.Block() as block:
            S.emit(block)
    return nc


_NAMES = ["pool_w_in", "pool_w_group", "pool_scale", "pool_w_out", "attn_w_qkv", "attn_w_out",
          "mlp_w_up", "mlp_w_down", "ln_mix_g", "ln_mix_b", "ln_mlp_g", "ln_mlp_b"]


def kernel(**inputs):
    x = np.ascontiguousarray(np.asarray(inputs["x"], dtype=np.float32))
    n = 8
    xs = x.reshape(n, 4096, D)
    shared = {k: np.ascontiguousarray(np.asarray(inputs[k], dtype=np.float32)) for k in _NAMES}
    nc = build_program()
    in_maps = []
    for c in range(n):
        m = {"x": xs[c]}
        m.update(shared)
        in_maps.append(m)
    res = run_bass_kernel_spmd(nc, in_maps, core_ids=list(range(n)))
    outs = [np.asarray(r["out"], dtype=np.float32).reshape(2, SEQ, D) for r in res.results]
    return np.concatenate(outs, axis=0)
```
